# Optimizing a Trainium2 kernel written in Bass

```python
import jax, jax.numpy as jnp
from jax import lax
import numpy as np

D_MODEL = 4096
BATCH = 1
SEQ = 8192
DEPTH = 1

PLE_DIM = 256
HEAD_DIM = 128
D_MIX = D_MODEL
D_GDN = D_MIX // 2
D_ATT = D_MIX - D_GDN
GDN_V_HEADS = D_GDN // HEAD_DIM
GDN_K_HEADS = GDN_V_HEADS // 2
GDN_KEY_DIM = GDN_K_HEADS * HEAD_DIM
CONV_WIDTH = 5
CONV_DIM = 2 * GDN_KEY_DIM + D_GDN
GDN_CHUNK = 64
ATT_HEADS = D_ATT // HEAD_DIM
DILATED_PATTERNS = ((128, 1), (512, 4), (2048, 16))
ROPE_THETA = 10000.0
LN_EPS = 1e-5
RMS_EPS = 1e-6
NEG_INF = -1e30

SPLITS = (GDN_KEY_DIM, GDN_KEY_DIM, D_GDN, D_GDN, 2 * GDN_V_HEADS, 2 * GDN_V_HEADS,
          D_ATT, D_ATT, D_ATT, D_ATT)
N_IN = sum(SPLITS)
SPLIT_POINTS = tuple(int(c) for c in np.cumsum(SPLITS)[:-1])

kernel_name = 'hybrid_gdn_dilated_attention_encoder_layer'


def layer_norm(t, g, b):
    tf = t.astype(jnp.float32)
    mu = jnp.mean(tf, axis=-1, keepdims=True)
    var = jnp.mean(jnp.square(tf - mu), axis=-1, keepdims=True)
    return ((tf - mu) * lax.rsqrt(var + LN_EPS) * g + b).astype(t.dtype)


def l2_normalize(t):
    return t * lax.rsqrt(jnp.sum(jnp.square(t), axis=-1, keepdims=True) + RMS_EPS)


def centred_depthwise_conv(u, w):
    width, chans = w.shape
    pad = (width - 1) // 2
    return lax.conv_general_dilated(u, w[:, None, :], window_strides=(1,), padding=[(pad, pad)],
                                    dimension_numbers=('NWC', 'WIO', 'NWC'),
                                    feature_group_count=chans)


def rotary(t, pos):
    half = t.shape[-1] // 2
    inv_freq = 1.0 / (jnp.float32(ROPE_THETA) ** (jnp.arange(half, dtype=jnp.float32) / half))
    ang = pos.astype(jnp.float32)[:, None] * inv_freq[None, :]
    cos, sin = jnp.cos(ang)[:, None, :], jnp.sin(ang)[:, None, :]
    tf = t.astype(jnp.float32)
    t1, t2 = tf[..., :half], tf[..., half:]
    return jnp.concatenate([t1 * cos - t2 * sin, t1 * sin + t2 * cos], axis=-1).astype(t.dtype)


def gated_delta_chunked(q, k, v, g, beta):
    B, H, S, Dk = k.shape
    Dv = v.shape[-1]
    C = GDN_CHUNK
    N = S // C
    q, k, v = (t.reshape(B, H, N, C, t.shape[-1]) for t in (q, k, v))
    g = jnp.cumsum(g.reshape(B, H, N, C), axis=-1)
    beta = beta.reshape(B, H, N, C, 1)
    k_beta = k * beta
    v_beta = v * beta
    incl = jnp.tril(jnp.ones((C, C), bool))
    strict = jnp.tril(jnp.ones((C, C), bool), -1)
    decay = jnp.exp(jnp.where(incl, g[..., :, None] - g[..., None, :], -jnp.inf))
    a_mat = jnp.where(strict, jnp.einsum('bhnid,bhnjd->bhnij', k_beta, k) * decay, 0.0)
    eye = jnp.eye(C, dtype=jnp.float32)
    rhs = jnp.concatenate([v_beta, k_beta * jnp.exp(g)[..., None]], axis=-1)
    sol = lax.linalg.triangular_solve(eye + a_mat, rhs, left_side=True, lower=True,
                                      unit_diagonal=True)
    u, w = sol[..., :Dv], sol[..., Dv:]
    qk = jnp.where(incl, jnp.einsum('bhnid,bhnjd->bhnij', q, k) * decay, 0.0)
    g_last = g[..., -1]
    k_tail = k * jnp.exp(g_last[..., None] - g)[..., None]
    q_dec = q * jnp.exp(g)[..., None]

    def step(state, xs):
        u_c, w_c, qk_c, kt_c, qd_c, gl_c = xs
        v_new = u_c - jnp.einsum('bhck,bhkv->bhcv', w_c, state)
        o_c = (jnp.einsum('bhck,bhkv->bhcv', qd_c, state)
               + jnp.einsum('bhcj,bhjv->bhcv', qk_c, v_new))
        state = (state * jnp.exp(gl_c)[..., None, None]
                 + jnp.einsum('bhck,bhcv->bhkv', kt_c, v_new))
        return state, o_c

    xs = tuple(jnp.moveaxis(t, 2, 0) for t in (u, w, qk, k_tail, q_dec, g_last))
    state0 = jnp.zeros((B, H, Dk, Dv), jnp.float32)
    _, o = lax.scan(step, state0, xs)
    return jnp.moveaxis(o, 0, 2).reshape(B, H, S, Dv)


def gdn_branch(qa, ka, va, za, aa, ba, conv_w, a_log, dt_bias, norm_w):
    B, S, _ = qa.shape
    qkv = jax.nn.silu(centred_depthwise_conv(jnp.concatenate([qa, ka, va], axis=-1), conv_w))
    qkv = qkv.astype(jnp.float32)
    q = qkv[..., :GDN_KEY_DIM].reshape(B, S, GDN_K_HEADS, HEAD_DIM)
    k = qkv[..., GDN_KEY_DIM:2 * GDN_KEY_DIM].reshape(B, S, GDN_K_HEADS, HEAD_DIM)
    v = qkv[..., 2 * GDN_KEY_DIM:].reshape(B, S, GDN_V_HEADS, HEAD_DIM)
    rep = GDN_V_HEADS // GDN_K_HEADS
    q = jnp.repeat(l2_normalize(q) * (HEAD_DIM ** -0.5), rep, axis=2).transpose(0, 2, 1, 3)
    k = jnp.repeat(l2_normalize(k), rep, axis=2).transpose(0, 2, 1, 3)
    v = v.transpose(0, 2, 1, 3)
    a = aa.astype(jnp.float32).reshape(B, S, 2, GDN_V_HEADS)
    b = ba.astype(jnp.float32).reshape(B, S, 2, GDN_V_HEADS)
    g = -jnp.exp(a_log.astype(jnp.float32)) * jax.nn.softplus(a + dt_bias.astype(jnp.float32))
    beta = jax.nn.sigmoid(b)
    g = g.transpose(2, 0, 3, 1)
    beta = beta.transpose(2, 0, 3, 1)
    flip = lambda t: jnp.flip(t, axis=2)
    o_fwd = gated_delta_chunked(q, k, v, g[0], beta[0])
    o_bwd = flip(gated_delta_chunked(flip(q), flip(k), flip(v), flip(g[1]), flip(beta[1])))
    o = (o_fwd + o_bwd).transpose(0, 2, 1, 3)
    o = o * lax.rsqrt(jnp.mean(jnp.square(o), axis=-1, keepdims=True) + RMS_EPS) * norm_w
    return (o.reshape(B, S, D_GDN) * jax.nn.silu(za.astype(jnp.float32))).astype(qa.dtype)


def banded_attention(q, k, v, radius):
    B, L, G, H, D = q.shape
    R = radius
    nb = -(-L // R)
    Lp = nb * R
    pad_q = Lp - L
    qp = jnp.pad(q, ((0, 0), (0, pad_q), (0, 0), (0, 0), (0, 0)))
    kp = jnp.pad(k, ((0, 0), (R, pad_q + R), (0, 0), (0, 0), (0, 0)))
    vp = jnp.pad(v, ((0, 0), (R, pad_q + R), (0, 0), (0, 0), (0, 0)))
    qb = qp.reshape(B, nb, R, G, H, D)

    def key_blocks(t):
        return jnp.concatenate([t[:, j * R:j * R + Lp].reshape(B, nb, R, G, H, D)
                                for j in range(3)], axis=2)

    kb, vb = key_blocks(kp), key_blocks(vp)
    s = jnp.einsum('bnqghd,bnkghd->bnghqk', qb, kb).astype(jnp.float32)
    blk = jnp.arange(nb)[:, None] * R
    qpos = blk + jnp.arange(R)[None, :]
    kpos = blk - R + jnp.arange(3 * R)[None, :]
    valid = ((jnp.abs(qpos[:, :, None] - kpos[:, None, :]) <= R)
             & (kpos[:, None, :] >= 0) & (kpos[:, None, :] < L))
    s = jnp.where(valid[None, :, None, None], s, NEG_INF)
    m = jnp.max(s, axis=-1, keepdims=True)
    e = jnp.exp(s - m)
    den = jnp.sum(e, axis=-1, keepdims=True)
    o = jnp.einsum('bnghqk,bnkghd->bnqghd', (e / den).astype(v.dtype), vb)
    lse = (m + jnp.log(den))[..., 0].transpose(0, 1, 4, 2, 3)
    o = o.reshape(B, Lp, G, H, D)[:, :L]
    lse = lse.reshape(B, Lp, G, H)[:, :L]
    return o, lse


def dilated_window_attention(q, k, v, dilation, radius):
    B, S, H, D = q.shape
    split = lambda t: t.reshape(B, S // dilation, dilation, H, D)
    o, lse = banded_attention(split(q), split(k), split(v), radius)
    return o.reshape(B, S, H, D), lse.reshape(B, S, H)


def dilated_attention_branch(qb, kb, vb, zb):
    B, S, _ = qb.shape
    pos = jnp.arange(S)
    q = rotary(qb.reshape(B, S, ATT_HEADS, HEAD_DIM), pos) * (HEAD_DIM ** -0.5)
    k = rotary(kb.reshape(B, S, ATT_HEADS, HEAD_DIM), pos)
    v = vb.reshape(B, S, ATT_HEADS, HEAD_DIM)
    outs, lses = [], []
    for window, dilation in DILATED_PATTERNS:
        o, lse = dilated_window_attention(q, k, v, dilation, window // (2 * dilation))
        outs.append(o)
        lses.append(lse)
    wts = jax.nn.softmax(jnp.stack(lses), axis=0)
    o = jnp.einsum('pbsh,pbshd->bshd', wts, jnp.stack(outs).astype(jnp.float32))
    return (o.reshape(B, S, D_ATT) * jax.nn.silu(zb.astype(jnp.float32))).astype(qb.dtype)


def hybrid_mixer(h, w_in, conv_w, a_log, dt_bias, gdn_norm_w, w_out):
    proj = jnp.einsum('bsd,dn->bsn', h, w_in)
    qa, ka, va, za, aa, ba, qb, kb, vb, zb = jnp.split(proj, SPLIT_POINTS, axis=-1)
    y_gdn = gdn_branch(qa, ka, va, za, aa, ba, conv_w, a_log, dt_bias, gdn_norm_w)
    y_att = dilated_attention_branch(qb, kb, vb, zb)
    y = jnp.concatenate([y_gdn, y_att], axis=-1)
    return jnp.einsum('bsm,md->bsd', y, w_out)


def setup_inputs(seed: int = 0) -> dict:
    key = jax.random.key(seed)
    ks = jax.random.split(key, 16)
    beta_init = (8 * DEPTH) ** -0.25
    offs = np.concatenate([[0], np.cumsum(SPLITS)])
    col_scale = np.ones((N_IN,), np.float32)
    col_scale[offs[2]:offs[3]] = beta_init
    col_scale[offs[8]:offs[9]] = beta_init
    nrm = jax.random.normal
    x = nrm(ks[0], (BATCH, SEQ, D_MODEL), jnp.float32)
    p = nrm(ks[1], (DEPTH, BATCH, SEQ, PLE_DIM), jnp.float32)
    w_in = nrm(ks[2], (DEPTH, D_MODEL, N_IN), jnp.float32) * (D_MODEL ** -0.5) * jnp.asarray(col_scale)
    conv_w = nrm(ks[3], (DEPTH, CONV_WIDTH, CONV_DIM), jnp.float32) * (CONV_WIDTH ** -0.5)
    a_log = jnp.log(jax.random.uniform(ks[4], (DEPTH, 2, GDN_V_HEADS), jnp.float32, 1.0, 16.0))
    dt = jnp.exp(jax.random.uniform(ks[5], (DEPTH, 2, GDN_V_HEADS), jnp.float32,
                                    float(np.log(1e-3)), float(np.log(1e-1))))
    dt_bias = dt + jnp.log(-jnp.expm1(-dt))
    gdn_norm_w = 1.0 + 0.02 * nrm(ks[6], (DEPTH, HEAD_DIM), jnp.float32)
    w_out = nrm(ks[7], (DEPTH, D_MIX, D_MODEL), jnp.float32) * (D_MIX ** -0.5) * beta_init
    ln1_g = 1.0 + 0.02 * nrm(ks[8], (DEPTH, D_MODEL), jnp.float32)
    ln1_b = 0.02 * nrm(ks[9], (DEPTH, D_MODEL), jnp.float32)
    w_ple_gate = nrm(ks[10], (DEPTH, D_MODEL, D_MODEL), jnp.float32) * (D_MODEL ** -0.5)
    w_ple_up = nrm(ks[11], (DEPTH, PLE_DIM, D_MODEL), jnp.float32) * (PLE_DIM ** -0.5) * beta_init
    ln2_g = 1.0 + 0.02 * nrm(ks[12], (DEPTH, D_MODEL), jnp.float32)
    ln2_b = 0.02 * nrm(ks[13], (DEPTH, D_MODEL), jnp.float32)
    return {'x': x, 'p': p, 'w_in': w_in, 'conv_w': conv_w, 'a_log': a_log,
            'dt_bias': dt_bias, 'gdn_norm_w': gdn_norm_w, 'w_out': w_out,
            'ln1_g': ln1_g, 'ln1_b': ln1_b, 'w_ple_gate': w_ple_gate, 'w_ple_up': w_ple_up,
            'ln2_g': ln2_g, 'ln2_b': ln2_b}


def reference(x, p, w_in, conv_w, a_log, dt_bias, gdn_norm_w, w_out, ln1_g, ln1_b,
              w_ple_gate, w_ple_up, ln2_g, ln2_b):
    alpha = (2 * DEPTH) ** 0.25
    h = x
    for i in range(DEPTH):
        mix = hybrid_mixer(h, w_in[i], conv_w[i], a_log[i], dt_bias[i], gdn_norm_w[i], w_out[i])
        h = layer_norm(alpha * h + mix, ln1_g[i], ln1_b[i])
        gate = jax.nn.sigmoid(jnp.einsum('bsd,de->bse', h, w_ple_gate[i]).astype(jnp.float32))
        ple = (gate * jnp.einsum('bsk,kd->bsd', p[i], w_ple_up[i]).astype(jnp.float32)).astype(h.dtype)
        h = layer_norm(alpha * h + ple, ln2_g[i], ln2_b[i])
    return h
```

```python
import contextlib
import numpy as np
import ml_dtypes
import concourse.bass as bass
import concourse.mybir as mybir
from concourse.bass_utils import run_bass_kernel_spmd

F32 = mybir.dt.float32
BF16 = mybir.dt.bfloat16
AF = mybir.ActivationFunctionType
ALU = mybir.AluOpType
AX = mybir.AxisListType

NCORES = 8
S = 8192
D = 4096
SPLITS = (1024, 1024, 2048, 2048, 32, 32, 2048, 2048, 2048, 2048)
OFFS = np.concatenate([[0], np.cumsum(SPLITS)]).astype(int)
ARENA = 48 * 1024 - 512
import os
GDN_STOP = int(os.environ.get('GDN_STOP', '9'))
FAST = int(os.environ.get('FAST', '0'))


class Buf:
    __slots__ = ("name", "last_w", "readers", "excl")

    def __init__(self, name="", excl=False):
        self.name = name
        self.last_w = None
        self.readers = []
        self.excl = excl


def pbuf(name=""):
    return Buf(name, excl=True)


class T:
    __slots__ = ("ap", "buf")

    def __init__(self, ap, buf=None):
        self.ap = ap
        self.buf = buf if buf is not None else Buf()

    def __getitem__(self, k):
        return T(self.ap[k], self.buf)

    def re(self, s, **kw):
        return T(self.ap.rearrange(s, **kw), self.buf)

    def bc(self, dt):
        return T(self.ap.bitcast(dt), self.buf)

    def on(self, buf):
        return T(self.ap, buf)


class Op:
    __slots__ = ("eng", "fn", "deps", "dma", "signal", "sem", "val", "idx")


class Prog:
    ENGS = ("pe", "act", "dve", "pool", "sp")
    LIMIT = 30000
    NDS = 24

    def __init__(self, nc):
        self.nc = nc
        self.ops = []
        self.last_dma_on_sem = [None] * self.NDS
        self.ndma = 0
        self.ndma_sw = 0
        self.pending_dma = []
        self.last_op = {e: None for e in self.ENGS}

    def add(self, eng, fn, reads=(), writes=(), dma=0, extra_deps=()):
        idx = len(self.ops)
        reads = [t.buf for t in reads if isinstance(t, T)]
        writes = [t.buf for t in writes if isinstance(t, T)]
        deps = set(extra_deps)
        raw = set()
        for b in reads:
            if b.last_w is not None:
                deps.add(b.last_w)
                raw.add(b.last_w)
            if b.excl:
                deps.update(r for r in b.readers if self.ops[r].eng != eng)
        for b in writes:
            if b.last_w is not None:
                deps.add(b.last_w)
            deps.update(b.readers)
        keep = set(extra_deps)
        for d in deps:
            p = self.ops[d]
            if p.dma or dma:
                keep.add(d)
                continue
            if p.eng == eng and eng == "pe":
                continue
            keep.add(d)
        op = Op()
        op.eng, op.fn, op.dma, op.idx = eng, fn, dma, idx
        op.signal = False
        op.sem = op.val = None
        if dma:
            if eng == "pool":
                slot = self.ndma_sw % 8
                self.ndma_sw += 1
            else:
                slot = 8 + self.ndma % (self.NDS - 8)
                self.ndma += 1
            prev = self.last_dma_on_sem[slot]
            if prev is not None:
                keep.add(prev)
            self.last_dma_on_sem[slot] = idx
            op.sem = slot
            self.pending_dma.append(idx)
        op.deps = sorted(keep)
        for d in op.deps:
            self.ops[d].signal = True
        self.ops.append(op)
        for b in reads:
            b.readers.append(idx)
        for b in writes:
            b.last_w = idx
            b.readers = []
        self.last_op[eng] = idx
        return idx

    def barrier(self):
        deps = [v for v in self.last_op.values() if v is not None] + list(self.pending_dma)
        self.pending_dma = []
        for e in self.ENGS:
            self.add(e, None, extra_deps=deps)

    @staticmethod
    def _a(x):
        return x.ap if isinstance(x, T) else x

    def matmul(self, out, lhsT, rhs, start=True, stop=True, xr=()):
        a = self._a
        return self.add("pe", lambda e: e.matmul(a(out), a(lhsT), a(rhs), start=start, stop=stop),
                        reads=[lhsT, rhs, *xr], writes=[out])

    def transpose(self, out, in_, ident):
        a = self._a
        return self.add("pe", lambda e: e.transpose(a(out), a(in_), a(ident)),
                        reads=[in_, ident], writes=[out])

    def act(self, out, in_, func, bias=None, scale=None, xw=()):
        a = self._a
        kw = {}
        if bias is not None:
            kw["bias"] = a(bias)
        if scale is not None:
            kw["scale"] = a(scale)
        return self.add("act", lambda e: e.activation(a(out), a(in_), func, **kw),
                        reads=[in_, bias, scale], writes=[out, *xw])

    def tt(self, eng, out, in0, in1, op, xr=(), xw=()):
        a = self._a
        return self.add(eng, lambda e: e.tensor_tensor(a(out), a(in0), a(in1), op),
                        reads=[in0, in1, *xr], writes=[out, *xw])

    def ts(self, eng, out, in0, s1, op0, s2=None, op1=None):
        a = self._a
        if op1 is None:
            return self.add(eng, lambda e: e.tensor_scalar(a(out), a(in0), a(s1), None, op0),
                            reads=[in0, s1], writes=[out])
        return self.add(eng, lambda e: e.tensor_scalar(a(out), a(in0), a(s1), a(s2), op0, op1),
                        reads=[in0, s1, s2], writes=[out])

    def stt(self, eng, out, in0, scalar, in1, op0, op1, xr=()):
        a = self._a
        return self.add(eng, lambda e: e.scalar_tensor_tensor(a(out), a(in0), a(scalar), a(in1), op0, op1),
                        reads=[in0, scalar, in1, *xr], writes=[out])

    def copy(self, eng, out, in_, xr=(), xw=()):
        a = self._a
        if eng == "act":
            return self.add(eng, lambda e: e.copy(a(out), a(in_)), reads=[in_, *xr], writes=[out, *xw])
        return self.add(eng, lambda e: e.tensor_copy(a(out), a(in_)), reads=[in_, *xr], writes=[out, *xw])

    def memset(self, eng, out, val):
        a = self._a
        return self.add(eng, lambda e: e.memset(a(out), val), writes=[out])

    def recip(self, out, in_):
        a = self._a
        return self.add("dve", lambda e: e.reciprocal(a(out), a(in_)), reads=[in_], writes=[out])

    def rsum(self, out, in_):
        a = self._a
        return self.add("dve", lambda e: e.reduce_sum(a(out), a(in_), AX.X), reads=[in_], writes=[out])

    def dma(self, eng, out, in_, **kw):
        a = self._a
        return self.add(eng, lambda e, sem: e.dma_start(out=a(out), in_=a(in_), **kw).then_inc(sem, 16),
                        reads=[in_], writes=[out], dma=1)

    def emit(self):
        nc = self.nc
        final_wait_ops = [i for i, o in enumerate(self.ops) if o.dma]
        cnt = {e: 0 for e in self.ENGS}
        semidx = {e: 0 for e in self.ENGS}
        dcount = [0] * self.NDS
        nsem = {e: 1 for e in self.ENGS}
        for op in self.ops:
            if op.dma:
                dcount[op.sem] += 16 * op.dma
                op.val = dcount[op.sem]
                op.sem = ("d", op.sem)
            elif op.signal and op.fn is not None:
                if cnt[op.eng] >= self.LIMIT:
                    cnt[op.eng] = 0
                    semidx[op.eng] += 1
                    nsem[op.eng] = semidx[op.eng] + 1
                cnt[op.eng] += 1
                op.sem = (op.eng, semidx[op.eng])
                op.val = cnt[op.eng]
        with contextlib.ExitStack() as st:
            sems = {}
            for e in self.ENGS:
                for i in range(nsem[e]):
                    sems[(e, i)] = st.enter_context(nc.semaphore(f"s_{e}{i}"))
            for i in range(self.NDS):
                sems[("d", i)] = st.enter_context(nc.semaphore(f"s_d{i}"))
            block = st.enter_context(nc.Block())
            engfun = {"pe": block.tensor, "act": block.scalar, "dve": block.vector,
                      "pool": block.gpsimd, "sp": block.sync}
            ops = self.ops

            def make(ename):
                def body(eng):
                    waited = {}

                    def wait_for(d):
                        p = ops[d]
                        if p.sem is None:
                            return
                        if waited.get(p.sem, 0) < p.val:
                            eng.wait_ge(sems[p.sem], p.val)
                            waited[p.sem] = p.val

                    for o in ops:
                        if o.eng != ename:
                            continue
                        for d in o.deps:
                            wait_for(d)
                        if o.fn is None:
                            continue
                        if o.dma:
                            o.fn(eng, sems[o.sem])
                        else:
                            ins = o.fn(eng)
                            if o.signal:
                                ins.then_inc(sems[o.sem], 1)
                    if ename == "sp":
                        for d in final_wait_ops:
                            wait_for(d)
                return body

            for e in self.ENGS:
                engfun[e](make(e))


class Arena:
    def __init__(self, tensor, ncols):
        self.t = tensor
        self.n = ncols
        self.p = 0

    def f32(self, n, name=""):
        off = self.p
        self.p += n
        assert self.p <= self.n, (self.p, self.n, name)
        return T(self.t[:, off:off + n], Buf(name))

    def bf16(self, n, name=""):
        nf = (n + 1) // 2
        off = self.p
        self.p += nf
        assert self.p <= self.n, (self.p, self.n, name)
        return T(self.t[:, off:off + nf].bitcast(BF16), Buf(name))


def v3(t, a):
    return t.re("p (a b) -> p a b", a=a)


C_ID, C_LT, C_LE, C_GT, C_GE, C_BO, C_ONE, C_BAND, C_N = 0, 128, 256, 384, 512, 640, 768, 896, 1152


def make_consts():
    r = np.arange(128)[:, None]
    c = np.arange(128)[None, :]
    same = (r // 64) == (c // 64)
    cst = np.zeros((128, C_N), np.float32)
    cst[:, C_ID:C_ID + 128] = (r == c)
    cst[:, C_LT:C_LT + 128] = same & (r > c)
    cst[:, C_LE:C_LE + 128] = same & (r >= c)
    cst[:, C_GT:C_GT + 128] = same & (r < c)
    cst[:, C_GE:C_GE + 128] = same & (r <= c)
    cst[:, C_BO:C_BO + 128] = same
    cst[:, C_ONE:C_ONE + 128] = 1.0
    w = np.arange(256)[None, :]
    cst[:, C_BAND:C_BAND + 256] = (w >= r) & (w <= r + 128)
    return cst


def rope_tables():
    half = 64
    inv_freq = (1.0 / (np.float32(10000.0) ** (np.arange(half, dtype=np.float32) / np.float32(half)))).astype(np.float32)
    ang = np.arange(S, dtype=np.float32)[:, None] * inv_freq[None, :]
    cos = np.cos(ang).astype(np.float32).T
    sin = np.sin(ang).astype(np.float32).T
    cosT = np.concatenate([cos, cos], 0)
    sinX = np.concatenate([sin, -sin], 0)
    return np.ascontiguousarray(cosT), np.ascontiguousarray(sinX)


def build_A(stages=("gdn", "att")):
    nc = bass.Bass("TRN2", target_bir_lowering=False)
    dt_in = lambda n, s: T(nc.dram_tensor(n, s, F32, kind="ExternalInput").ap(), Buf(n))
    xT = dt_in("xT", [D, S])
    w1 = dt_in("w1", [D, 1792])
    wg = dt_in("wg", [D, 8])
    convw = dt_in("convw", [128, 20])
    alog = dt_in("alog", [128, 4])
    dtb = dt_in("dtb", [128, 4])
    gnw = dt_in("gnw", [128, 128])
    cosT = dt_in("cosT", [128, S])
    sinX = dt_in("sinX", [128, S])
    cst_d = dt_in("cst", [128, C_N])
    yT = T(nc.dram_tensor("yT", [512, S], BF16, kind="ExternalOutput").ap(), Buf("yT"))
    U = T(nc.dram_tensor("U", [4, 128, S + 4], F32, kind="Internal").ap(), Buf("U"))
    ZS = T(nc.dram_tensor("ZS", [4, 128, S], BF16, kind="Internal").ap(), Buf("ZS"))
    QKV = T(nc.dram_tensor("QKV", [6, 128, S], BF16, kind="Internal").ap(), Buf("QKV"))

    P = Prog(nc)
    with contextlib.ExitStack() as st:
        arena_t = st.enter_context(nc.sbuf_tensor("arena", [128, ARENA], F32))
        ps_t = st.enter_context(nc.psum_tensor("ps", [128, 4096], F32))
        A = Arena(arena_t, ARENA)
        bank = lambda k: ps_t[:, 512 * k:512 * (k + 1)]

        cst = A.f32(C_N, "cst")
        cstb = A.bf16(C_N, "cstb")
        G = A.f32(64 * 8, "G")
        small = A.f32(20 + 4 + 4 + 128 + 4, "small")
        convw_s = small[:, 0:20]
        alog_s = small[:, 20:24]
        dtb_s = small[:, 24:28]
        gnw_s = small[:, 28:156]
        eps6 = small[:, 156:157]
        zero2 = small[:, 158:160]
        P.dma("sp", cst, cst_d)
        P.dma("sp", convw_s, convw)
        P.dma("sp", alog_s, alog)
        P.dma("sp", dtb_s, dtb)
        P.dma("sp", gnw_s, gnw)
        P.copy("dve", cstb, cst)
        P.memset("dve", eps6, 1e-6)
        P.memset("dve", zero2, 0.0)
        ident32 = cst[:, C_ID:C_ID + 128]
        identb = cstb[:, C_ID:C_ID + 128]
        onesb = cstb[:, C_ONE:C_ONE + 128]
        bandb = cstb[:, C_BAND:C_BAND + 256]
        base_mark = A.p

        TB = 256
        NTB = S // TB
        xTv = xT.re("(kc p) t -> p kc t", p=128)
        w1v = w1.re("(kc p) f -> p kc f", p=128)
        wgv = wg.re("(kc p) f -> p kc f", p=128)
        Wg = [v3(A.bf16(4 * 1792, f"W{i}"), 4) for i in range(8)]
        WG = v3(A.bf16(32 * 8, "WG"), 32)
        XB = [[v3(A.bf16(16 * TB, f"XB{i}{j}"), 16) for j in range(2)] for i in range(2)]
        CS = [v3(A.f32(2 * TB, f"CS{i}"), 2) for i in range(2)]
        stf = [A.f32(TB, f"stf{i}") for i in range(4)]
        stb = [A.bf16(TB, f"stb{i}") for i in range(8)]
        tm1 = [A.f32(TB, f"tm1{i}") for i in range(2)]
        tm2 = [A.f32(TB, f"tm2{i}") for i in range(2)]
        PSF = [T(bank(k)[:, 0:TB], pbuf(f"psf{k}")) for k in range(7)]
        PSG = T(bank(7)[:, 0:16], pbuf("psg"))
        for i in range(8):
            P.dma("pool", Wg[i], w1v[:, 4 * i:4 * i + 4, :])
        P.dma("pool", WG, wgv)
        Uv = U.re("c p t -> p c t")
        zt = A.f32(8, "zt")
        P.memset("pool", zt, 0.0)
        P.dma("sp", Uv[:, :, 0:2], v3(zt, 4))
        P.dma("sp", Uv[:, :, S + 2:S + 4], v3(zt, 4))

        def load_block(tb):
            for j in range(2):
                P.dma("pool", XB[tb % 2][j], xTv[:, 16 * j:16 * j + 16, tb * TB:(tb + 1) * TB])
            P.dma("sp", CS[tb % 2][:, 0, :], cosT[:, tb * TB:(tb + 1) * TB])
            P.dma("sp", CS[tb % 2][:, 1, :], sinX[:, tb * TB:(tb + 1) * TB])

        load_block(0)
        nf = nb = nt = 0
        KCS = list(range(32)) if not FAST else [0]
        for tb in range(NTB):
            if tb + 1 < NTB:
                load_block(tb + 1)
            xb = XB[tb % 2]
            cs = CS[tb % 2]
            tsl = slice(tb * TB, (tb + 1) * TB)
            for ft in range(14):
                ps = PSF[(tb * 14 + ft) % 7]
                for kc in KCS:
                    P.matmul(ps, Wg[kc // 4][:, kc % 4, ft * 128:(ft + 1) * 128], xb[kc // 16][:, kc % 16, :],
                             start=(kc == KCS[0]), stop=(kc == KCS[-1]))
                if ft < 4:
                    s_ = stf[nf % 4]; nf += 1
                    P.copy("act", s_, ps)
                    P.dma("sp", U[ft][:, 2 + tb * TB:2 + (tb + 1) * TB], s_)
                elif ft in (4, 5, 12, 13):
                    s_ = stb[nb % 8]; nb += 1
                    P.act(s_, ps, AF.Silu)
                    zi = ft - 4 if ft < 6 else ft - 10
                    P.dma("sp", ZS[zi][:, tsl], s_)
                elif ft in (10, 11):
                    s_ = stb[nb % 8]; nb += 1
                    P.copy("dve", s_, ps)
                    P.dma("sp", QKV[4 + ft - 10][:, tsl], s_)
                else:
                    t1 = tm1[nt % 2]; t2 = tm2[nt % 2]; nt += 1
                    s_ = stb[nb % 8]; nb += 1
                    P.tt("dve", t1, ps, cs[:, 0, :], ALU.mult)
                    P.tt("dve", t2[0:64, :], ps[64:128, :], cs[64:128, 1, :], ALU.mult)
                    P.tt("dve", t2[64:128, :], ps[0:64, :], cs[0:64, 1, :], ALU.mult)
                    P.tt("pool", s_, t1, t2, ALU.add)
                    P.dma("sp", QKV[ft - 6][:, tsl], s_)
            for hf in range(2):
                for kc in KCS:
                    P.matmul(PSG[:, hf * 8:hf * 8 + 8], xb[kc // 16][:, kc % 16, hf * 128:(hf + 1) * 128],
                             WG[:, kc, :], start=(kc == KCS[0]), stop=(kc == KCS[-1]))
            P.copy("act", G[:, tb * 16:tb * 16 + 16], PSG)
        P.barrier()
        A.p = base_mark

        if "gdn" in stages:
            build_gdn(nc, P, A, bank, locals())
            P.barrier()
            A.p = base_mark
        if "att" in stages:
            build_att(nc, P, A, bank, locals())
        P.emit()
    return nc


def build_att(nc, P, A, bank, env):
    QKV, ZS, yT = env["QKV"], env["ZS"], env["yT"]
    identb, onesb, bandb = env["identb"], env["onesb"], env["bandb"]
    QT = A.bf16(S, "aQT")
    KT = A.bf16(S, "aKT")
    VT = A.bf16(S, "aVT")
    ZG = A.bf16(S, "aZG")
    OA = A.f32(S, "aOA")
    DN = A.f32(S, "aDN")
    PT = [A.bf16(256, f"aPT{i}") for i in range(3)]
    PM = [A.bf16(256, f"aPM{i}") for i in range(3)]
    VK = [A.bf16(128, f"aVK{i}") for i in range(3)]
    YB = [A.bf16(512, f"aYB{i}") for i in range(2)]
    PS_S = [T(bank(k)[:, 0:256], pbuf(f"pss{k}")) for k in range(2)]
    PS_V = [T(bank(2 + k)[:, 0:64].bitcast(BF16), pbuf(f"psv{k}")) for k in range(2)]
    PS_O = [T(bank(4 + k)[:, 0:256], pbuf(f"pso{k}")) for k in range(2)]
    PS_D = [T(bank(6 + k)[:, 0:256], pbuf(f"psd{k}")) for k in range(2)]
    it = 0
    for h in range(2):
        P.dma("sp", QT, QKV[0 + h])
        P.dma("sp", KT, QKV[2 + h])
        P.dma("sp", VT, QKV[4 + h])
        P.dma("sp", ZG, ZS[2 + h])
        for q4 in range(4):
            P.memset("pool", OA[:, q4 * 2048:(q4 + 1) * 2048], 0.0)
            P.memset("pool", DN[:, q4 * 2048:(q4 + 1) * 2048], 0.0)
        for dil in (1, 4, 16):
            L = S // dil
            for r in range(dil):
                for kt in range(L // 128):
                    wlo = max(0, 128 * kt - 64)
                    whi = min(L, 128 * kt + 192)
                    nW = whi - wlo
                    off = wlo - (128 * kt - 64)
                    k0 = r + dil * 128 * kt
                    ksl = slice(k0, k0 + dil * 127 + 1, dil)
                    q0 = r + dil * wlo
                    qsl = slice(q0, q0 + dil * (nW - 1) + 1, dil)
                    pss = PS_S[it % 2]; psv = PS_V[it % 2]; pso = PS_O[it % 2]; psd = PS_D[it % 2]
                    pt = PT[it % 3]; pm = PM[it % 3]; vk = VK[it % 3]
                    it += 1
                    P.matmul(pss[:, 0:nW], KT[:, ksl], QT[:, qsl])
                    P.transpose(psv, VT[:, ksl], identb)
                    P.act(pt[:, 0:nW], pss[:, 0:nW], AF.Exp, scale=float(128 ** -0.5))
                    P.copy("act", vk, psv)
                    P.tt("pool", pm[:, 0:nW], pt[:, 0:nW], bandb[:, off:off + nW], ALU.mult)
                    P.matmul(pso[:, 0:nW], vk, pm[:, 0:nW])
                    P.matmul(psd[:, 0:nW], onesb, pm[:, 0:nW])
                    P.tt("dve", OA[:, qsl], pso[:, 0:nW], OA[:, qsl], ALU.add)
                    P.tt("dve", DN[:, qsl], psd[:, 0:nW], DN[:, qsl], ALU.add)
        for q in range(16):
            sl = slice(q * 512, (q + 1) * 512)
            yb = YB[q % 2]
            P.recip(DN[:, sl], DN[:, sl])
            P.tt("pool", OA[:, sl], OA[:, sl], DN[:, sl], ALU.mult)
            P.tt("dve", yb, OA[:, sl], ZG[:, sl], ALU.mult)
            P.dma("sp", yT[256 + h * 128:256 + (h + 1) * 128, sl], yb)


def build_gdn(nc, P, A, bank, env):
    U, ZS, yT, G = env["U"], env["ZS"], env["yT"], env["G"]
    cst, cstb = env["cst"], env["cstb"]
    ident32, identb, onesb = env["ident32"], env["identb"], env["onesb"]
    convw_s, alog_s, dtb_s, gnw_s, eps6 = env["convw_s"], env["alog_s"], env["dtb_s"], env["gnw_s"], env["eps6"]
    QT = A.bf16(S, "gQT")
    KT = A.bf16(S, "gKT")
    Ktok = v3(A.bf16(64 * 128, "gKtok"), 64)
    Vtok = v3(A.bf16(64 * 256, "gVtok"), 64)
    OtA = A.f32(64 * 256, "gOt")
    Ot = [T(OtA.ap[:, t * 256:(t + 1) * 256], Buf(f"Ot{t}")) for t in range(64)]
    work_mark = A.p
    mini = env.get("mini")
    if mini:
        return build_gdn_core(nc, P, A, bank, env, locals())

    UB = [A.f32(516, f"ub{i}") for i in range(2)]
    ACC = [A.f32(512, f"acc{i}") for i in range(2)]
    SB = [A.f32(512, f"sb{i}") for i in range(2)]
    SQ = [A.bf16(512, f"sq{i}") for i in range(2)]
    RS = [A.f32(512, f"rs{i}") for i in range(2)]
    PSN = [T(bank(k), pbuf(f"psn{k}")) for k in range(2)]
    PSTF = [T(bank(2 + k), pbuf(f"pstf{k}")) for k in range(2)]
    PSTB = [T(bank(4 + k)[:, 0:256].bitcast(BF16), pbuf(f"pstb{k}")) for k in range(2)]
    it = 0
    for ct in range(4):
        for tb in range(16):
            ub, acc, sb, sq, rs = UB[it % 2], ACC[it % 2], SB[it % 2], SQ[it % 2], RS[it % 2]
            eng = "dve"
            P.dma("sp", ub, U[ct][:, tb * 512:tb * 512 + 516])
            P.ts(eng, acc, ub[:, 0:512], convw_s[:, ct * 5:ct * 5 + 1], ALU.mult, 0.0, ALU.add)
            for k in range(1, 5):
                P.stt(eng, acc, ub[:, k:k + 512], convw_s[:, ct * 5 + k:ct * 5 + k + 1], acc, ALU.mult, ALU.add)
            P.act(sb, acc, AF.Silu)
            sl = slice(tb * 512, (tb + 1) * 512)
            if ct < 2:
                P.tt("pool", sq, sb, sb, ALU.mult)
                psn = PSN[it % 2]
                P.matmul(psn, onesb, sq)
                P.act(rs, psn, AF.Sqrt, bias=eps6)
                P.recip(rs, rs)
                dst = QT if ct == 0 else KT
                P.stt("dve", dst[:, sl], sb, float(128 ** -0.5) if ct == 0 else 1.0, rs, ALU.mult, ALU.mult)
                if ct == 1:
                    pst = PSTB[it % 2]
                    for j in range(4):
                        P.transpose(pst[:, j * 128:(j + 1) * 128], KT[:, tb * 512 + j * 128:tb * 512 + (j + 1) * 128], identb)
                    P.copy("act", Ktok[:, 4 * tb:4 * tb + 4, :], v3(pst, 4))
            else:
                h = ct - 2
                pst = PSTF[it % 2]
                for j in range(4):
                    P.transpose(pst[:, j * 128:(j + 1) * 128], sb[:, j * 128:(j + 1) * 128], ident32)
                P.copy("act", Vtok[:, 4 * tb:4 * tb + 4, h * 128:(h + 1) * 128], v3(pst, 4))
            it += 1
    P.barrier()
    A.p = work_mark
    if GDN_STOP == 0:
        return

    g4 = v3(A.f32(256, "g4"), 64)
    beta4 = v3(A.f32(256, "beta4"), 64)
    tmpg = v3(A.f32(256, "tmpg"), 64)
    nA = A.f32(4, "nA")
    G3 = v3(G, 64)
    P.act(nA, alog_s, AF.Exp)
    P.ts("dve", nA, nA, -1.0, ALU.mult, 0.0, ALU.add)
    for s_ in range(4):
        P.ts("dve", tmpg[:, :, s_], G3[:, :, s_], dtb_s[:, s_:s_ + 1], ALU.add, 0.0, ALU.add)
    P.act(tmpg, tmpg, AF.Exp)
    P.act(tmpg, tmpg, AF.Ln, bias=1.0)
    for s_ in range(4):
        P.ts("dve", g4[:, :, s_], tmpg[:, :, s_], nA[:, s_:s_ + 1], ALU.mult, 0.0, ALU.add)
    P.act(beta4, G3[:, :, 4:8], AF.Sigmoid)
    P.barrier()
    if GDN_STOP == 1:
        return
    return build_gdn_core(nc, P, A, bank, env, locals())


def build_gdn_core(nc, P, A, bank, env, loc):
    U, ZS, yT, G = env["U"], env["ZS"], env["yT"], env["G"]
    cst, cstb = env["cst"], env["cstb"]
    ident32, identb, onesb = env["ident32"], env["identb"], env["onesb"]
    gnw_s, eps6 = env["gnw_s"], env["eps6"]
    QT, KT, Ktok, Vtok, Ot, work_mark = (loc[k] for k in ("QT", "KT", "Ktok", "Vtok", "Ot", "work_mark"))
    env["Ot"] = Ot
    mini = env.get("mini")
    if mini:
        rnd = env["rnd"]
        g4 = v3(A.f32(256, "g4"), 64)
        beta4 = v3(A.f32(256, "beta4"), 64)
        rs_ = T(loc["OtA"].ap[:, 8192:12288], Buf("rnds"))
        P.dma("sp", rs_, rnd)
        P.copy("dve", QT[:, 0:256], rs_[:, 0:256])
        P.copy("dve", KT[:, 0:256], rs_[:, 256:512])
        P.copy("dve", Ktok[:, 0:2, :], v3(rs_[:, 512:768], 2))
        P.copy("dve", Vtok[:, 0:2, :], v3(rs_[:, 768:1280], 2))
        P.copy("dve", g4[:, 0:2, :], v3(rs_[:, 1280:1288], 2))
        P.copy("dve", beta4[:, 0:2, :], v3(rs_[:, 1288:1296], 2))
        P.barrier()
    else:
        g4, beta4 = loc["g4"], loc["beta4"]

    KQm = [A.f32(256, f"KQm{d}") for d in range(2)]
    Lg = [A.f32(128, f"Lg{s_}") for s_ in range(4)]
    E = [A.f32(128, f"E{s_}") for s_ in range(4)]
    STa = [[A.bf16(384, f"ST{s_}{k}") for k in range(2)] for s_ in range(4)]
    STxy = [[STa[s_][k].on(Buf(f"STxy{s_}{k}")) for k in range(2)] for s_ in range(4)]
    STp = [[STa[s_][k].on(Buf(f"STp{s_}{k}")) for k in range(2)] for s_ in range(4)]
    AQ = [[A.bf16(256, f"AQ{p}{s_}") for s_ in range(4)] for p in range(2)]
    Y0 = [A.bf16(128, f"Y0{s_}") for s_ in range(4)]
    TTb = [[A.bf16(128, f"TTb{p}{s_}") for s_ in range(4)] for p in range(2)]
    ktb = [[A.bf16(128, f"ktb{p}{s_}") for s_ in range(4)] for p in range(2)]
    P5b = [A.bf16(128, f"P5b{s_}") for s_ in range(4)]
    sc = [[A.f32(16, f"sc{p}{d}") for d in range(2)] for p in range(2)]
    gsel = [A.f32(4, f"gsel{d}") for d in range(2)]
    Sbf = [A.bf16(256, f"Sbf{d}") for d in range(2)]
    S32 = [A.f32(256, f"S32{d}") for d in range(2)]
    rb = [A.bf16(256, f"rb{d}") for d in range(2)]
    qs = [A.f32(256, f"qs{d}") for d in range(2)]
    vn = [A.bf16(256, f"vn{d}") for d in range(2)]
    tmpo = [A.f32(256, f"tmpo{d}") for d in range(2)]
    for d in range(2):
        P.memset("pool", gsel[d], 0.0)
        P.memset("pool", Sbf[d], 0.0)
        P.memset("pool", S32[d], 0.0)
    bb = [pbuf(f"gbank{k}") for k in range(8)]
    PKS = [T(bank(2 * d)[:, 0:256], bb[2 * d]) for d in range(2)]
    PSU = [T(bank(2 * d)[:, 256:512], bb[2 * d]) for d in range(2)]
    PQS = [T(bank(2 * d + 1)[:, 0:256], bb[2 * d + 1]) for d in range(2)]
    PVO = [T(bank(2 * d + 1)[:, 256:512], bb[2 * d + 1]) for d in range(2)]
    PK = T(bank(4)[:, 0:256], bb[4])
    PSM = [T(bank(4)[:, 256 + 8 * d:264 + 8 * d], bb[4]) for d in range(2)]
    PD = [T(bank(5)[:, 128 * h:128 * h + 128], bb[5]) for h in range(2)]
    PCB = [T(bank(5)[:, 256 + 64 * h:320 + 64 * h].bitcast(BF16), bb[5]) for h in range(2)]
    PCH = [T(bank(6 + h)[:, 0:384], bb[6 + h]) for h in range(2)]
    mLT, mLE, mGT, mGE = (cst[:, o:o + 128] for o in (C_LT, C_LE, C_GT, C_GE))
    mBO, mONE = cst[:, C_BO:C_BO + 128], cst[:, C_ONE:C_ONE + 128]
    MstrictT = [mLT, mGT]
    Mincl = [mGE, mLE]
    Mcat = [cst[:, C_GT:C_GT + 256], cst[:, C_LT:C_LT + 256]]

    def g1(tile, d, par):
        kt = KT[:, tile * 128:(tile + 1) * 128]
        qt = QT[:, tile * 128:(tile + 1) * 128]
        scd = sc[par][d]
        gcol = g4[:, tile, 2 * d:2 * d + 2]
        bcol = beta4[:, tile, 2 * d:2 * d + 2]
        P.matmul(PK[:, 0:128], kt, kt)
        P.matmul(PK[:, 128:256], kt, qt)
        P.tt("dve", KQm[d], PK, Mcat[d], ALU.mult)
        yield
        psm = PSM[d]
        P.matmul(psm[:, 0:2], MstrictT[d], gcol)
        P.matmul(psm[:, 2:4], mBO, gcol)
        gs3 = v3(gsel[d], 2)
        P.copy("pool", gs3[0:64, :, 0], gcol[0:64, :])
        P.copy("pool", gs3[64:128, :, 1], gcol[64:128, :])
        P.matmul(psm[:, 4:8], mONE, gsel[d])
        P.copy("act", scd[:, 0:2], psm[:, 0:2])
        P.tt("dve", scd[:, 2:4], psm[:, 2:4], scd[:, 0:2], ALU.subtract)
        P.act(scd[:, 4:6], scd[:, 2:4], AF.Exp)
        P.ts("dve", scd[:, 6:8], scd[:, 4:6], -1.0, ALU.mult, 0.0, ALU.add)
        P.act(scd[:, 8:10], scd[:, 0:2], AF.Exp)
        P.tt("dve", scd[:, 8:10], scd[:, 8:10], bcol, ALU.mult)
        P.act(scd[:, 10:14], psm[:, 4:8], AF.Exp)
        yield
        for h in range(2):
            s_ = 2 * d + h
            b1 = beta4[:, tile, s_:s_ + 1]
            P.ts("pool", Lg[s_], Mincl[d], g4[:, tile, s_:s_ + 1], ALU.mult, 0.0, ALU.add)
            P.matmul(PD[h], MstrictT[d], Lg[s_])
            P.act(E[s_], PD[h], AF.Exp)
            aq = AQ[par][s_]
            P.stt("dve", v3(aq, 2), T(E[s_].ap.unsqueeze(1).to_broadcast([128, 2, 128]), E[s_].buf), b1,
                  v3(KQm[d], 2), ALU.mult, ALU.mult)
            P.ts("pool", ktb[par][s_], Ktok[:, tile, :], scd[:, 8 + h:9 + h], ALU.mult, 0.0, ALU.add)
            yield
            X0 = aq[:, 0:128]
            pcb = PCB[h]
            pch = PCH[h]
            P.transpose(pcb, X0, identb)
            P.copy("act", Y0[s_], pcb)
            st1 = STa[s_][1]
            P.tt("pool", st1[:, 128:256].on(STp[s_][1].buf), identb, X0, ALU.subtract)
            P.matmul(pch[:, 0:128], Y0[s_], X0)
            P.matmul(pch[:, 256:384], X0, Y0[s_])
            xyv = lambda t_: T(t_.ap.rearrange("p (a b) -> p a b", a=3)[:, 0:3:2, :], t_.buf)
            P.copy("act", xyv(st1.on(STxy[s_][1].buf)), xyv(pch))
            yield
            for k in range(1, 5):
                cur = STa[s_][k % 2]
                nxt = STa[s_][(k + 1) % 2]
                bxy, bp = STxy[s_][k % 2].buf, STp[s_][k % 2].buf
                nxy, np_ = STxy[s_][(k + 1) % 2].buf, STp[s_][(k + 1) % 2].buf
                SUB = int(os.environ.get("G1_SUB", "9"))
                if SUB >= 1:
                    P.matmul(pch[:, 0:256], cur[:, 256:384].on(bxy), cur[:, 0:256].on(bxy), xr=[cur.on(bp)])
                if SUB >= 2:
                    P.matmul(pch[:, 256:384], cur[:, 0:128].on(bxy), cur[:, 256:384].on(bxy))
                if SUB >= 3:
                    P.copy("act", xyv(nxt.on(nxy)), xyv(pch))
                if SUB >= 4:
                    P.tt("dve", nxt[:, 128:256].on(np_), pch[:, 128:256], cur[:, 128:256].on(bp), ALU.add)
                yield
            cur = STa[s_][1]
            P.matmul(pch[:, 128:256], cur[:, 256:384].on(STxy[s_][1].buf), cur[:, 128:256].on(STp[s_][1].buf))
            P.tt("dve", TTb[par][s_], pch[:, 128:256], cur[:, 128:256].on(STp[s_][1].buf), ALU.add)
            yield

    def g2(tile, d, par, first):
        kt = KT[:, tile * 128:(tile + 1) * 128]
        qt = QT[:, tile * 128:(tile + 1) * 128]
        scd = sc[par][d]
        for c in ((0, 1) if d == 0 else (1, 0)):
            R = slice(64 * c, 64 * c + 64)
            P.matmul(PKS[d], kt, Sbf[d])
            P.matmul(PQS[d], qt, Sbf[d])
            for h in range(2):
                hs = slice(h * 128, (h + 1) * 128)
                P.stt("dve", rb[d][R, hs], PKS[d][R, hs], scd[R, 6 + h:7 + h], Vtok[R, tile, hs], ALU.mult, ALU.add)
            for h in range(2):
                hs = slice(h * 128, (h + 1) * 128)
                P.act(qs[d][R, hs], PQS[d][R, hs], AF.Identity, scale=scd[R, 4 + h:5 + h])
            yield
            for h in range(2):
                hs = slice(h * 128, (h + 1) * 128)
                P.matmul(PVO[d][:, hs], TTb[par][2 * d + h][R, :], rb[d][R, hs])
            P.copy("act", vn[d][R, :], PVO[d][R, :])
            yield
            for h in range(2):
                hs = slice(h * 128, (h + 1) * 128)
                P.matmul(PSU[d][:, hs], ktb[par][2 * d + h][R, :], vn[d][R, hs])
            for h in range(2):
                hs = slice(h * 128, (h + 1) * 128)
                P.matmul(PVO[d][:, hs], AQ[par][2 * d + h][R, 128:256], vn[d][R, hs])
            for h in range(2):
                hs = slice(h * 128, (h + 1) * 128)
                eg = scd[:, 10 + 2 * h + c:11 + 2 * h + c]
                P.stt("dve", Sbf[d][:, hs], S32[d][:, hs], eg, PSU[d][:, hs], ALU.mult, ALU.add)
            for h in range(2):
                hs = slice(h * 128, (h + 1) * 128)
                eg = scd[:, 10 + 2 * h + c:11 + 2 * h + c]
                P.stt("dve", S32[d][:, hs], S32[d][:, hs], eg, PSU[d][:, hs], ALU.mult, ALU.add)
            if first:
                P.tt("dve", Ot[tile][R, :], PVO[d][R, :], qs[d][R, :], ALU.add)
            else:
                P.tt("dve", tmpo[d][R, :], PVO[d][R, :], qs[d][R, :], ALU.add)
                P.tt("pool", Ot[tile][R, :], Ot[tile][R, :], tmpo[d][R, :], ALU.add)
            yield

    def run_round_robin(gens):
        gens = list(gens)
        while gens:
            for g_ in list(gens):
                try:
                    next(g_)
                except StopIteration:
                    gens.remove(g_)

    NT = 64 if not mini else 2
    if GDN_STOP == 2:
        import itertools
        n1 = int(os.environ.get('G1_STOP', '99'))
        for g_ in (g1(0, 0, 0), g1(NT - 1, 1, 0)):
            for _ in itertools.islice(g_, n1):
                pass
        return
    run_round_robin([g1(0, 0, 0), g1(NT - 1, 1, 0)])
    for i in range(NT):
        par = i % 2
        gens = [g2(i, 0, par, i < NT // 2), g2(NT - 1 - i, 1, par, i < NT // 2)]
        if i + 1 < NT:
            gens += [g1(i + 1, 0, 1 - par), g1(NT - 2 - i, 1, 1 - par)]
        run_round_robin(gens)
    P.barrier()
    A.p = work_mark
    if GDN_STOP == 3 or mini:
        return

    sqb = [A.f32(256, f"g3sq{i}") for i in range(2)]
    ssb = [A.f32(2, f"g3ss{i}") for i in range(2)]
    onb = [A.f32(256, f"g3on{i}") for i in range(2)]
    zg = [v3(A.bf16(1024, f"g3zg{i}"), 2) for i in range(2)]
    yb = [v3(A.bf16(256, f"g3yb{i}"), 2) for i in range(2)]
    PT3 = [T(bank(k)[:, 0:256], pbuf(f"pt3{k}")) for k in range(2)]
    yTv = yT[0:256, :].re("(h p) t -> p h t", h=2)
    ZSv = T(ZS.ap[0:2].rearrange("h p t -> p h t"), ZS.buf)
    for tile in range(NT):
        i2 = tile % 2
        if tile % 4 == 0:
            zt_ = zg[(tile // 4) % 2]
            P.dma("sp", zt_, ZSv[:, :, tile * 128:tile * 128 + 512])
        zt_ = zg[(tile // 4) % 2]
        o = Ot[tile]
        P.tt("pool", sqb[i2], o, o, ALU.mult)
        P.rsum(ssb[i2], v3(sqb[i2], 2))
        P.act(ssb[i2], ssb[i2], AF.Sqrt, bias=eps6, scale=1.0 / 128.0)
        P.recip(ssb[i2], ssb[i2])
        for h in range(2):
            hs = slice(h * 128, (h + 1) * 128)
            P.stt("dve", onb[i2][:, hs], o[:, hs], ssb[i2][:, h:h + 1], gnw_s, ALU.mult, ALU.mult)
            P.transpose(PT3[i2][:, hs], onb[i2][:, hs], ident32)
        P.tt("dve", yb[i2], v3(PT3[i2], 2), zt_[:, :, (tile % 4) * 128:(tile % 4) * 128 + 128], ALU.mult)
        P.dma("sp", yTv[:, :, tile * 128:(tile + 1) * 128], yb[i2])


def core_cols(c):
    o = OFFS
    cols = np.concatenate([
        np.arange(o[0] + 128 * c, o[0] + 128 * c + 128),
        np.arange(o[1] + 128 * c, o[1] + 128 * c + 128),
        np.arange(o[2] + 256 * c, o[2] + 256 * c + 256),
        np.arange(o[3] + 256 * c, o[3] + 256 * c + 256),
        np.arange(o[6] + 256 * c, o[6] + 256 * c + 256),
        np.arange(o[7] + 256 * c, o[7] + 256 * c + 256),
        np.arange(o[8] + 256 * c, o[8] + 256 * c + 256),
        np.arange(o[9] + 256 * c, o[9] + 256 * c + 256)])
    gc = np.array([o[4] + 2 * c, o[4] + 2 * c + 1, o[4] + 16 + 2 * c, o[4] + 16 + 2 * c + 1,
                   o[5] + 2 * c, o[5] + 2 * c + 1, o[5] + 16 + 2 * c, o[5] + 16 + 2 * c + 1])
    return cols, gc


def prep_A(inputs):
    x = np.asarray(inputs["x"], np.float32)[0]
    w_in = np.asarray(inputs["w_in"], np.float32)[0]
    conv_w = np.asarray(inputs["conv_w"], np.float32)[0]
    a_log = np.asarray(inputs["a_log"], np.float32)[0]
    dt_bias = np.asarray(inputs["dt_bias"], np.float32)[0]
    gnw = np.asarray(inputs["gdn_norm_w"], np.float32)[0]
    xT = np.ascontiguousarray(x.T)
    cosT, sinX = rope_tables()
    cst = make_consts()
    maps = []
    for c in range(NCORES):
        cols, gcols = core_cols(c)
        ch = np.concatenate([np.arange(128 * c, 128 * c + 128), 1024 + np.arange(128 * c, 128 * c + 128),
                             2048 + np.arange(256 * c, 256 * c + 256)])
        cw = conv_w[:, ch].reshape(5, 4, 128).transpose(2, 1, 0).reshape(128, 20)
        st = [(0, 2 * c), (0, 2 * c + 1), (1, 2 * c), (1, 2 * c + 1)]
        al = np.array([a_log[d, h] for d, h in st], np.float32)
        db = np.array([dt_bias[d, h] for d, h in st], np.float32)
        maps.append({
            "xT": xT,
            "w1": np.ascontiguousarray(w_in[:, cols]),
            "wg": np.ascontiguousarray(w_in[:, gcols]),
            "convw": np.ascontiguousarray(cw),
            "alog": np.ascontiguousarray(np.broadcast_to(al, (128, 4))),
            "dtb": np.ascontiguousarray(np.broadcast_to(db, (128, 4))),
            "gnw": np.ascontiguousarray(np.broadcast_to(gnw, (128, 128))),
            "cosT": cosT, "sinX": sinX, "cst": cst,
        })
    return maps


def run_A(inputs, stages=("gdn", "att")):
    nc = build_A(stages)
    res = run_bass_kernel_spmd(nc, prep_A(inputs), core_ids=list(range(NCORES)))
    return [np.asarray(r["yT"]) for r in res.results]


def build_mini():
    nc = bass.Bass("TRN2", target_bir_lowering=False)
    rnd = T(nc.dram_tensor("rnd", [128, 4096], F32, kind="ExternalInput").ap())
    cst_d = T(nc.dram_tensor("cst", [128, C_N], F32, kind="ExternalInput").ap())
    ot = T(nc.dram_tensor("ot", [128, 512], F32, kind="ExternalOutput").ap())
    P = Prog(nc)
    with contextlib.ExitStack() as st:
        arena_t = st.enter_context(nc.sbuf_tensor("arena", [128, ARENA], F32))
        ps_t = st.enter_context(nc.psum_tensor("ps", [128, 4096], F32))
        A = Arena(arena_t, ARENA)
        bank = lambda k: ps_t[:, 512 * k:512 * (k + 1)]
        cst = A.f32(C_N, "cst")
        cstb = A.bf16(C_N, "cstb")
        small = A.f32(160, "small")
        P.dma("sp", cst, cst_d)
        P.copy("dve", cstb, cst)
        env = dict(U=None, ZS=None, yT=None, G=None, cst=cst, cstb=cstb, ident32=cst[:, C_ID:C_ID + 128],
                   identb=cstb[:, C_ID:C_ID + 128], onesb=cstb[:, C_ONE:C_ONE + 128], convw_s=None, alog_s=None,
                   dtb_s=None, gnw_s=None, eps6=None, mini=True, rnd=rnd)
        loc = build_gdn(nc, P, A, bank, env)
        P.barrier()
        if GDN_STOP != 2:
            for t in range(2):
                P.dma("sp", ot[:, t * 256:(t + 1) * 256], env["Ot"][t])
        P.emit()
    return nc


ALPHA = float(2.0 ** 0.25)


def emit_B(nc, P, A, bank, yTa, x_c, pT_c, wo, wgt, wup, lng, out_c):
    NB = 256
    NNB = D // NB
    YT = v3(A.bf16(32 * 512, "bYT"), 32)
    H = v3(A.f32(4 * D, "bH"), 4)
    WB = [v3(A.bf16(32 * NB, f"bWB{i}"), 32) for i in range(2)]
    WU = [v3(A.bf16(2 * NB, f"bWU{i}"), 2) for i in range(2)]
    PTb = v3(A.bf16(2 * 512, "bPT"), 2)
    XS = [v3(A.f32(4 * NB, f"bXS{i}"), 4) for i in range(2)]
    GB = [[A.f32(512, f"bGB{i}{j}") for j in range(2)] for i in range(2)]
    SG = [A.f32(NB, f"bSG{i}") for i in range(4)]
    PL = [A.f32(NB, f"bPL{i}") for i in range(4)]
    junk = A.bf16(512, "bjunk")
    stat = A.f32(64, "bstat")
    epsln = A.f32(1, "bepsln")
    ident32 = A.f32(128, "bident")
    P.memset("dve", epsln, 1e-5)
    P.memset("pool", ident32, 0.0)
    P.dma("sp", ident32, lng["ident"])
    PSB = [T(bank(k)[:, 0:NB], pbuf(f"psb{k}")) for k in range(8)]
    PSX = [T(bank(k), pbuf(f"psx{k}")) for k in range(8)]
    yv = yTa.re("(kc p) t -> p kc t", p=128)
    wov = wo.re("(kc p) n -> p kc n", p=128)
    wgv = wgt.re("(kc p) n -> p kc n", p=128)
    wuv = wup.re("(kc p) n -> p kc n", p=128)
    pv = pT_c.re("(kc p) t -> p kc t", p=128)
    xv = x_c.re("(tt p) n -> p tt n", p=128)
    ov = out_c.re("(tt p) n -> p tt n", p=128)
    cnt = [0]

    def layer_norm(gname, bname):
        for tt in range(4):
            h = H[:, tt, :]
            st_ = stat[:, tt * 16:(tt + 1) * 16]
            P.rsum(st_[:, 0:1], h)
            for j in range(8):
                P.add("act", lambda e, j=j, h=h, st_=st_: e.activation(junk.ap, h.ap[:, j * 512:(j + 1) * 512], AF.Square,
                                                                     accum_out=st_.ap[:, 2 + j:3 + j]),
                      reads=[h], writes=[junk, st_])
            P.rsum(st_[:, 1:2], st_[:, 2:10])
            P.ts("dve", st_[:, 10:11], st_[:, 0:1], 1.0 / D, ALU.mult, 0.0, ALU.add)
            P.ts("dve", st_[:, 11:12], st_[:, 1:2], 1.0 / D, ALU.mult, 0.0, ALU.add)
            P.tt("dve", st_[:, 12:13], st_[:, 10:11], st_[:, 10:11], ALU.mult)
            P.tt("dve", st_[:, 11:12], st_[:, 11:12], st_[:, 12:13], ALU.subtract)
            P.act(st_[:, 13:14], st_[:, 11:12], AF.Sqrt, bias=epsln)
            P.recip(st_[:, 13:14], st_[:, 13:14])
            P.ts("dve", h, h, st_[:, 10:11], ALU.subtract, st_[:, 13:14], ALU.mult)
        for blk in range(8):
            gb = GB[blk % 2]
            sl = slice(blk * 512, (blk + 1) * 512)
            P.dma("sp", gb[0], lng[gname][:, sl])
            P.dma("sp", gb[1], lng[bname][:, sl])
            for tt in range(4):
                P.tt("pool", H[:, tt, sl], H[:, tt, sl], gb[0], ALU.mult)
                P.tt("dve", H[:, tt, sl], H[:, tt, sl], gb[1], ALU.add)

    for hp in range(2):
        t0 = hp * 512
        P.dma("sp", YT, yv[:, :, t0:t0 + 512])
        for nb in range(NNB):
            wb = WB[cnt[0] % 2]; xs = XS[cnt[0] % 2]; cnt[0] += 1
            nsl = slice(nb * NB, (nb + 1) * NB)
            P.dma("pool", wb, wov[:, :, nsl])
            P.dma("sp", xs, xv[:, 4 * hp:4 * hp + 4, nsl])
            for tt in range(4):
                ps = PSB[(nb * 4 + tt) % 8]
                for kc in range(32):
                    P.matmul(ps, YT[:, kc, tt * 128:(tt + 1) * 128], wb[:, kc, :], start=(kc == 0), stop=(kc == 31))
                P.stt("dve", H[:, tt, nsl], xs[:, tt, :], ALPHA, ps, ALU.mult, ALU.add)
        layer_norm("ln1g", "ln1b")
        for kc in range(32):
            px = PSX[kc % 8]
            for tt in range(4):
                P.transpose(px[:, tt * 128:(tt + 1) * 128], H[:, tt, kc * 128:(kc + 1) * 128], ident32)
            P.copy("act" if kc % 2 == 0 else "dve", YT[:, kc, :], px)
        P.dma("pool", PTb, pv[:, :, t0:t0 + 512])
        for nb in range(NNB):
            wb = WB[cnt[0] % 2]; wu = WU[cnt[0] % 2]; cnt[0] += 1
            nsl = slice(nb * NB, (nb + 1) * NB)
            P.dma("pool", wb, wgv[:, :, nsl])
            P.dma("pool", wu, wuv[:, :, nsl])
            for tt in range(4):
                pg = PSB[tt]
                pu = PSB[4 + tt]
                tsl = slice(tt * 128, (tt + 1) * 128)
                for kc in range(32):
                    P.matmul(pg, YT[:, kc, tsl], wb[:, kc, :], start=(kc == 0), stop=(kc == 31))
                for kc in range(2):
                    P.matmul(pu, PTb[:, kc, tsl], wu[:, kc, :], start=(kc == 0), stop=(kc == 1))
                sg = SG[tt]; pl = PL[tt]
                P.act(sg, pg, AF.Sigmoid)
                P.tt("dve", pl, pu, sg, ALU.mult)
                P.stt("dve", H[:, tt, nsl], H[:, tt, nsl], ALPHA, pl, ALU.mult, ALU.add)
        layer_norm("ln2g", "ln2b")
        for tt in range(4):
            P.dma("sp", ov[:, 4 * hp + tt, :], H[:, tt, :])


def build_B():
    nc = bass.Bass("TRN2", target_bir_lowering=False)
    din = lambda n, s, dt=F32: T(nc.dram_tensor(n, s, dt, kind="ExternalInput").ap(), Buf(n))
    yTa = din("yTa", [D, 1024], BF16)
    x_c = din("x_c", [1024, D])
    pT_c = din("pT_c", [256, 1024])
    wo = din("wo", [D, D])
    wgt = din("wgt", [D, D])
    wup = din("wup", [256, D])
    lng = {k: din(k, [128, D]) for k in ("ln1g", "ln1b", "ln2g", "ln2b")}
    lng["ident"] = din("ident", [128, 128])
    out_c = T(nc.dram_tensor("out_c", [1024, D], F32, kind="ExternalOutput").ap(), Buf("out_c"))
    P = Prog(nc)
    with contextlib.ExitStack() as st:
        arena_t = st.enter_context(nc.sbuf_tensor("arena", [128, ARENA], F32))
        ps_t = st.enter_context(nc.psum_tensor("ps", [128, 4096], F32))
        A = Arena(arena_t, ARENA)
        bank = lambda k: ps_t[:, 512 * k:512 * (k + 1)]
        emit_B(nc, P, A, bank, yTa, x_c, pT_c, wo, wgt, wup, lng, out_c)
        P.emit()
    return nc


def prep_B(inputs, yTs):
    x = np.asarray(inputs["x"], np.float32)[0]
    p = np.asarray(inputs["p"], np.float32)[0, 0]
    w_out = np.asarray(inputs["w_out"], np.float32)[0]
    wgt = np.ascontiguousarray(np.asarray(inputs["w_ple_gate"], np.float32)[0])
    wup = np.ascontiguousarray(np.asarray(inputs["w_ple_up"], np.float32)[0])
    rows = np.concatenate([np.concatenate([np.arange(256 * r, 256 * r + 256), 2048 + np.arange(256 * r, 256 * r + 256)])
                           for r in range(NCORES)])
    wo = np.ascontiguousarray(w_out[rows])
    yall = np.concatenate(yTs, 0)
    bc = lambda k: np.ascontiguousarray(np.broadcast_to(np.asarray(inputs[k], np.float32)[0], (128, D)))
    lnp = {"ln1g": bc("ln1_g"), "ln1b": bc("ln1_b"), "ln2g": bc("ln2_g"), "ln2b": bc("ln2_b")}
    ident = np.eye(128, dtype=np.float32)
    pT = np.ascontiguousarray(p.T)
    maps = []
    for c in range(NCORES):
        sl = slice(1024 * c, 1024 * (c + 1))
        maps.append({"yTa": np.ascontiguousarray(yall[:, sl]), "x_c": np.ascontiguousarray(x[sl]),
                     "pT_c": np.ascontiguousarray(pT[:, sl]), "wo": wo, "wgt": wgt, "wup": wup,
                     "ident": ident, **lnp})
    return maps


def kernel(**inputs):
    yTs = run_A(inputs)
    ncB = build_B()
    res = run_bass_kernel_spmd(ncB, prep_B(inputs, yTs), core_ids=list(range(NCORES)))
    out = np.concatenate([np.asarray(r["out_c"]) for r in res.results], 0)
    return out.reshape(1, S, D).astype(np.float32)
```

```python
import contextlib
import os
import numpy as np
import ml_dtypes
import concourse.bass as bass
import concourse.mybir as mybir
from concourse.bass_utils import run_bass_kernel_spmd

F32 = mybir.dt.float32
BF16 = mybir.dt.bfloat16
AF = mybir.ActivationFunctionType
ALU = mybir.AluOpType
AX = mybir.AxisListType

NCORES = 8
S = 8192
D = 4096
SPLITS = (1024, 1024, 2048, 2048, 32, 32, 2048, 2048, 2048, 2048)
OFFS = np.concatenate([[0], np.cumsum(SPLITS)]).astype(int)
ARENA = 48 * 1024 - 512
import os
GDN_STOP = int(os.environ.get('GDN_STOP', '9'))
FAST = int(os.environ.get('FAST', '0'))


class Buf:
    __slots__ = ("name", "last_w", "readers", "excl")

    def __init__(self, name="", excl=False):
        self.name = name
        self.last_w = None
        self.readers = []
        self.excl = excl


def pbuf(name=""):
    return Buf(name, excl=not int(os.environ.get("NOEXCL", "0")))


class T:
    __slots__ = ("ap", "buf")

    def __init__(self, ap, buf=None):
        self.ap = ap
        self.buf = buf if buf is not None else Buf()

    def __getitem__(self, k):
        return T(self.ap[k], self.buf)

    def re(self, s, **kw):
        return T(self.ap.rearrange(s, **kw), self.buf)

    def bc(self, dt):
        return T(self.ap.bitcast(dt), self.buf)

    def on(self, buf):
        return T(self.ap, buf)


class Op:
    __slots__ = ("eng", "fn", "deps", "dma", "signal", "sem", "val", "idx")


class Prog:
    ENGS = ("pe", "act", "dve", "pool", "sp")
    LIMIT = 30000
    NDS = 24

    def __init__(self, nc):
        self.nc = nc
        self.ops = []
        self.last_dma_on_sem = [None] * self.NDS
        self.ndma = 0
        self.ndma_sw = 0
        self.pending_dma = []
        self.last_op = {e: None for e in self.ENGS}

    def add(self, eng, fn, reads=(), writes=(), dma=0, extra_deps=()):
        idx = len(self.ops)
        reads = [t.buf for t in reads if isinstance(t, T)]
        writes = [t.buf for t in writes if isinstance(t, T)]
        deps = set(extra_deps)
        raw = set()
        for b in reads:
            if b.last_w is not None:
                deps.add(b.last_w)
                raw.add(b.last_w)
            if b.excl:
                deps.update(r for r in b.readers if self.ops[r].eng != eng)
        for b in writes:
            if b.last_w is not None:
                deps.add(b.last_w)
            deps.update(b.readers)
        keep = set(extra_deps)
        for d in deps:
            p = self.ops[d]
            if p.dma or dma:
                keep.add(d)
                continue
            if p.eng == eng and eng == "pe":
                continue
            keep.add(d)
        op = Op()
        op.eng, op.fn, op.dma, op.idx = eng, fn, dma, idx
        op.signal = False
        op.sem = op.val = None
        if dma:
            if eng == "pool":
                slot = self.ndma_sw % 8
                self.ndma_sw += 1
            else:
                slot = 8 + self.ndma % (self.NDS - 8)
                self.ndma += 1
            prev = self.last_dma_on_sem[slot]
            if prev is not None:
                keep.add(prev)
            self.last_dma_on_sem[slot] = idx
            op.sem = slot
            self.pending_dma.append(idx)
        op.deps = sorted(keep)
        for d in op.deps:
            self.ops[d].signal = True
        self.ops.append(op)
        for b in reads:
            b.readers.append(idx)
        for b in writes:
            b.last_w = idx
            b.readers = []
        self.last_op[eng] = idx
        return idx

    def barrier(self):
        deps = [v for v in self.last_op.values() if v is not None] + list(self.pending_dma)
        self.pending_dma = []
        for e in self.ENGS:
            self.add(e, None, extra_deps=deps)

    @staticmethod
    def _a(x):
        return x.ap if isinstance(x, T) else x

    def matmul(self, out, lhsT, rhs, start=True, stop=True, xr=()):
        a = self._a
        return self.add("pe", lambda e: e.matmul(a(out), a(lhsT), a(rhs), start=start, stop=stop),
                        reads=[lhsT, rhs, *xr], writes=[out])

    def transpose(self, out, in_, ident):
        a = self._a
        return self.add("pe", lambda e: e.transpose(a(out), a(in_), a(ident)),
                        reads=[in_, ident], writes=[out])

    def act(self, out, in_, func, bias=None, scale=None, xw=()):
        a = self._a
        kw = {}
        if bias is not None:
            kw["bias"] = a(bias)
        if scale is not None:
            kw["scale"] = a(scale)
        return self.add("act", lambda e: e.activation(a(out), a(in_), func, **kw),
                        reads=[in_, bias, scale], writes=[out, *xw])

    def tt(self, eng, out, in0, in1, op, xr=(), xw=()):
        a = self._a
        return self.add(eng, lambda e: e.tensor_tensor(a(out), a(in0), a(in1), op),
                        reads=[in0, in1, *xr], writes=[out, *xw])

    def ts(self, eng, out, in0, s1, op0, s2=None, op1=None):
        a = self._a
        if op1 is None:
            return self.add(eng, lambda e: e.tensor_scalar(a(out), a(in0), a(s1), None, op0),
                            reads=[in0, s1], writes=[out])
        return self.add(eng, lambda e: e.tensor_scalar(a(out), a(in0), a(s1), a(s2), op0, op1),
                        reads=[in0, s1, s2], writes=[out])

    def stt(self, eng, out, in0, scalar, in1, op0, op1, xr=()):
        a = self._a
        return self.add(eng, lambda e: e.scalar_tensor_tensor(a(out), a(in0), a(scalar), a(in1), op0, op1),
                        reads=[in0, scalar, in1, *xr], writes=[out])

    def copy(self, eng, out, in_, xr=(), xw=()):
        a = self._a
        if eng == "act":
            return self.add(eng, lambda e: e.copy(a(out), a(in_)), reads=[in_, *xr], writes=[out, *xw])
        return self.add(eng, lambda e: e.tensor_copy(a(out), a(in_)), reads=[in_, *xr], writes=[out, *xw])

    def memset(self, eng, out, val):
        a = self._a
        return self.add(eng, lambda e: e.memset(a(out), val), writes=[out])

    def recip(self, out, in_):
        a = self._a
        return self.add("dve", lambda e: e.reciprocal(a(out), a(in_)), reads=[in_], writes=[out])

    def rsum(self, out, in_):
        a = self._a
        return self.add("dve", lambda e: e.reduce_sum(a(out), a(in_), AX.X), reads=[in_], writes=[out])

    def collective(self, fn, reads, writes):
        idx = self.add("pool", fn, reads=reads, writes=writes, dma=1)
        self.ops[idx].sem = "cc"
        return idx

    def dma(self, eng, out, in_, **kw):
        a = self._a
        return self.add(eng, lambda e, sem: e.dma_start(out=a(out), in_=a(in_), **kw).then_inc(sem, 16),
                        reads=[in_], writes=[out], dma=1)

    def emit(self):
        nc = self.nc
        final_wait_ops = [i for i, o in enumerate(self.ops) if o.dma]
        cnt = {e: 0 for e in self.ENGS}
        semidx = {e: 0 for e in self.ENGS}
        dcount = {i: 0 for i in range(self.NDS)}
        dcount["cc"] = 0
        nsem = {e: 1 for e in self.ENGS}
        for op in self.ops:
            if op.dma:
                dcount[op.sem] += 1 if op.sem == "cc" else 16 * op.dma
                op.val = dcount[op.sem]
                op.sem = ("d", op.sem)
            elif op.signal and op.fn is not None:
                if cnt[op.eng] >= self.LIMIT:
                    cnt[op.eng] = 0
                    semidx[op.eng] += 1
                    nsem[op.eng] = semidx[op.eng] + 1
                cnt[op.eng] += 1
                op.sem = (op.eng, semidx[op.eng])
                op.val = cnt[op.eng]
        with contextlib.ExitStack() as st:
            sems = {}
            for e in self.ENGS:
                for i in range(nsem[e]):
                    sems[(e, i)] = st.enter_context(nc.semaphore(f"s_{e}{i}"))
            for i in range(self.NDS):
                sems[("d", i)] = st.enter_context(nc.semaphore(f"s_d{i}"))
            sems[("d", "cc")] = st.enter_context(nc.semaphore("s_cc"))
            block = st.enter_context(nc.Block())
            engfun = {"pe": block.tensor, "act": block.scalar, "dve": block.vector,
                      "pool": block.gpsimd, "sp": block.sync}
            ops = self.ops

            def make(ename):
                def body(eng):
                    waited = {}

                    def wait_for(d):
                        p = ops[d]
                        if p.sem is None:
                            return
                        if waited.get(p.sem, 0) < p.val:
                            eng.wait_ge(sems[p.sem], p.val)
                            waited[p.sem] = p.val

                    for o in ops:
                        if o.eng != ename:
                            continue
                        for d in o.deps:
                            wait_for(d)
                        if o.fn is None:
                            continue
                        if o.dma:
                            o.fn(eng, sems[o.sem])
                        else:
                            ins = o.fn(eng)
                            if o.signal:
                                ins.then_inc(sems[o.sem], 1)
                    if ename == "sp":
                        for d in final_wait_ops:
                            wait_for(d)
                return body

            for e in self.ENGS:
                engfun[e](make(e))


class Arena:
    def __init__(self, tensor, ncols):
        self.t = tensor
        self.n = ncols
        self.p = 0

    def f32(self, n, name=""):
        off = self.p
        self.p += n
        assert self.p <= self.n, (self.p, self.n, name)
        return T(self.t[:, off:off + n], Buf(name))

    def bf16(self, n, name=""):
        nf = (n + 1) // 2
        off = self.p
        self.p += nf
        assert self.p <= self.n, (self.p, self.n, name)
        return T(self.t[:, off:off + nf].bitcast(BF16), Buf(name))


def v3(t, a):
    return t.re("p (a b) -> p a b", a=a)


C_ID, C_LT, C_LE, C_GT, C_GE, C_BO, C_ONE, C_BAND, C_N = 0, 128, 256, 384, 512, 640, 768, 896, 1152


def make_consts():
    r = np.arange(128)[:, None]
    c = np.arange(128)[None, :]
    same = (r // 64) == (c // 64)
    cst = np.zeros((128, C_N), np.float32)
    cst[:, C_ID:C_ID + 128] = (r == c)
    cst[:, C_LT:C_LT + 128] = same & (r > c)
    cst[:, C_LE:C_LE + 128] = same & (r >= c)
    cst[:, C_GT:C_GT + 128] = same & (r < c)
    cst[:, C_GE:C_GE + 128] = same & (r <= c)
    cst[:, C_BO:C_BO + 128] = same
    cst[:, C_ONE:C_ONE + 128] = 1.0
    w = np.arange(256)[None, :]
    cst[:, C_BAND:C_BAND + 256] = (w >= r) & (w <= r + 128)
    return cst


def rope_tables():
    half = 64
    inv_freq = (1.0 / (np.float32(10000.0) ** (np.arange(half, dtype=np.float32) / np.float32(half)))).astype(np.float32)
    ang = np.arange(S, dtype=np.float32)[:, None] * inv_freq[None, :]
    cos = np.cos(ang).astype(np.float32).T
    sin = np.sin(ang).astype(np.float32).T
    cosT = np.concatenate([cos, cos], 0)
    sinX = np.concatenate([sin, -sin], 0)
    return np.ascontiguousarray(cosT), np.ascontiguousarray(sinX)


def build_A(stages=("gdn", "att"), fused=False):
    nc = bass.Bass("TRN2", target_bir_lowering=False)
    dt_in = lambda n, s: T(nc.dram_tensor(n, s, F32, kind="ExternalInput").ap(), Buf(n))
    xT = dt_in("xT", [D, S])
    w1 = dt_in("w1", [D, 1792])
    wg = dt_in("wg", [D, 8])
    convw = dt_in("convw", [128, 20])
    alog = dt_in("alog", [128, 4])
    dtb = dt_in("dtb", [128, 4])
    gnw = dt_in("gnw", [128, 128])
    cosT = dt_in("cosT", [128, S])
    sinX = dt_in("sinX", [128, S])
    cst_d = dt_in("cst", [128, C_N])
    if fused:
        yloc = nc.dram_tensor("yloc", [512, S // 2], F32, kind="Internal").ap()
        yall = nc.dram_tensor("yall", [D, S // 2], F32, kind="Internal", addr_space="Shared").ap()
        yT = T(yloc.bitcast(BF16), Buf("yT"))
        din = lambda n, s, dt=F32: T(nc.dram_tensor(n, s, dt, kind="ExternalInput").ap(), Buf(n))
        x_c = din("x_c", [1024, D])
        pT_c = din("pT_c", [256, 1024])
        wo = din("wo", [D, D])
        wgt = din("wgt", [D, D])
        wup = din("wup", [256, D])
        lng = {k: din(k, [128, D]) for k in ("ln1g", "ln1b", "ln2g", "ln2b")}
        lng["ident"] = din("ident", [128, 128])
        out_c = T(nc.dram_tensor("out_c", [1024, D], F32, kind="ExternalOutput").ap(), Buf("out_c"))
    else:
        yT = T(nc.dram_tensor("yT", [512, S], BF16, kind="ExternalOutput").ap(), Buf("yT"))
    U = T(nc.dram_tensor("U", [4, 128, S + 4], F32, kind="Internal").ap(), Buf("U"))
    ZS = T(nc.dram_tensor("ZS", [4, 128, S], BF16, kind="Internal").ap(), Buf("ZS"))
    QKV = T(nc.dram_tensor("QKV", [6, 128, S], BF16, kind="Internal").ap(), Buf("QKV"))

    P = Prog(nc)
    with contextlib.ExitStack() as st:
        arena_t = st.enter_context(nc.sbuf_tensor("arena", [128, ARENA], F32))
        ps_t = st.enter_context(nc.psum_tensor("ps", [128, 4096], F32))
        A = Arena(arena_t, ARENA)
        bank = lambda k: ps_t[:, 512 * k:512 * (k + 1)]

        cst = A.f32(C_N, "cst")
        cstb = A.bf16(C_N, "cstb")
        G = A.f32(64 * 8, "G")
        small = A.f32(20 + 4 + 4 + 128 + 4, "small")
        convw_s = small[:, 0:20]
        alog_s = small[:, 20:24]
        dtb_s = small[:, 24:28]
        gnw_s = small[:, 28:156]
        eps6 = small[:, 156:157]
        zero2 = small[:, 158:160]
        P.dma("sp", cst, cst_d)
        P.dma("sp", convw_s, convw)
        P.dma("sp", alog_s, alog)
        P.dma("sp", dtb_s, dtb)
        P.dma("sp", gnw_s, gnw)
        P.copy("dve", cstb, cst)
        nidentb = A.bf16(128, "nidentb")
        P.ts("dve", nidentb, cstb[:, C_ID:C_ID + 128], -1.0, ALU.mult, 0.0, ALU.add)
        P.memset("dve", eps6, 1e-6)
        P.memset("dve", zero2, 0.0)
        ident32 = cst[:, C_ID:C_ID + 128]
        identb = cstb[:, C_ID:C_ID + 128]
        onesb = cstb[:, C_ONE:C_ONE + 128]
        bandb = cstb[:, C_BAND:C_BAND + 256]
        base_mark = A.p

        TB = 256
        NTB = S // TB
        xTv = xT.re("(kc p) t -> p kc t", p=128)
        w1v = w1.re("(kc p) f -> p kc f", p=128)
        wgv = wg.re("(kc p) f -> p kc f", p=128)
        Wg = [v3(A.bf16(4 * 1792, f"W{i}"), 4) for i in range(8)]
        WG = v3(A.bf16(32 * 8, "WG"), 32)
        XB = [[v3(A.bf16(16 * TB, f"XB{i}{j}"), 16) for j in range(2)] for i in range(2)]
        CS = [v3(A.f32(2 * TB, f"CS{i}"), 2) for i in range(2)]
        stf = [A.f32(TB, f"stf{i}") for i in range(4)]
        stb = [A.bf16(TB, f"stb{i}") for i in range(8)]
        tm1 = [A.f32(TB, f"tm1{i}") for i in range(2)]
        tm2 = [A.f32(TB, f"tm2{i}") for i in range(2)]
        PSF = [T(bank(k)[:, 0:TB], pbuf(f"psf{k}")) for k in range(7)]
        PSG = T(bank(7)[:, 0:16], pbuf("psg"))
        for i in range(8):
            P.dma("pool", Wg[i], w1v[:, 4 * i:4 * i + 4, :])
        P.dma("pool", WG, wgv)
        Uv = U.re("c p t -> p c t")
        zt = A.f32(8, "zt")
        P.memset("pool", zt, 0.0)
        P.dma("sp", Uv[:, :, 0:2], v3(zt, 4))
        P.dma("sp", Uv[:, :, S + 2:S + 4], v3(zt, 4))

        def load_block(tb):
            for j in range(2):
                P.dma("pool", XB[tb % 2][j], xTv[:, 16 * j:16 * j + 16, tb * TB:(tb + 1) * TB])
            P.dma("sp", CS[tb % 2][:, 0, :], cosT[:, tb * TB:(tb + 1) * TB])
            P.dma("sp", CS[tb % 2][:, 1, :], sinX[:, tb * TB:(tb + 1) * TB])

        load_block(0)
        nf = nb = nt = 0
        KCS = list(range(32)) if not FAST else [0]
        for tb in range(NTB):
            if tb + 1 < NTB:
                load_block(tb + 1)
            xb = XB[tb % 2]
            cs = CS[tb % 2]
            tsl = slice(tb * TB, (tb + 1) * TB)
            for ft in range(14):
                ps = PSF[(tb * 14 + ft) % 7]
                for kc in KCS:
                    P.matmul(ps, Wg[kc // 4][:, kc % 4, ft * 128:(ft + 1) * 128], xb[kc // 16][:, kc % 16, :],
                             start=(kc == KCS[0]), stop=(kc == KCS[-1]))
                if ft < 4:
                    s_ = stf[nf % 4]; nf += 1
                    P.copy("act", s_, ps)
                    P.dma("sp", U[ft][:, 2 + tb * TB:2 + (tb + 1) * TB], s_)
                elif ft in (4, 5, 12, 13):
                    s_ = stb[nb % 8]; nb += 1
                    P.act(s_, ps, AF.Silu)
                    zi = ft - 4 if ft < 6 else ft - 10
                    P.dma("sp", ZS[zi][:, tsl], s_)
                elif ft in (10, 11):
                    s_ = stb[nb % 8]; nb += 1
                    P.copy("dve", s_, ps)
                    P.dma("sp", QKV[4 + ft - 10][:, tsl], s_)
                else:
                    t1 = tm1[nt % 2]; t2 = tm2[nt % 2]; nt += 1
                    s_ = stb[nb % 8]; nb += 1
                    P.tt("dve", t1, ps, cs[:, 0, :], ALU.mult)
                    P.tt("dve", t2[0:64, :], ps[64:128, :], cs[64:128, 1, :], ALU.mult)
                    P.tt("dve", t2[64:128, :], ps[0:64, :], cs[0:64, 1, :], ALU.mult)
                    P.tt("pool", s_, t1, t2, ALU.add)
                    P.dma("sp", QKV[ft - 6][:, tsl], s_)
            for hf in range(2):
                for kc in KCS:
                    P.matmul(PSG[:, hf * 8:hf * 8 + 8], xb[kc // 16][:, kc % 16, hf * 128:(hf + 1) * 128],
                             WG[:, kc, :], start=(kc == KCS[0]), stop=(kc == KCS[-1]))
            P.copy("act", G[:, tb * 16:tb * 16 + 16], PSG)
        P.barrier()
        A.p = base_mark

        if "gdn" in stages:
            build_gdn(nc, P, A, bank, locals())
            P.barrier()
            A.p = base_mark
        if "att" in stages:
            build_att(nc, P, A, bank, locals())
        if fused:
            P.barrier()
            A.p = 0
            YALL = T(yall, Buf("yall"))
            P.collective(lambda e, sem: e.collective_compute("AllGather", ALU.bypass, replica_groups=[list(range(NCORES))],
                                                            ins=[yloc], outs=[yall]).then_inc(sem, 1),
                         reads=[yT], writes=[YALL])
            emit_B(nc, P, A, bank, T(yall.bitcast(BF16), YALL.buf), x_c, pT_c, wo, wgt, wup, lng, out_c, dyn=True)
        P.emit()
    return nc


def build_att(nc, P, A, bank, env):
    QKV, ZS, yT = env["QKV"], env["ZS"], env["yT"]
    identb, onesb, bandb = env["identb"], env["onesb"], env["bandb"]
    QT = A.bf16(S, "aQT")
    KT = A.bf16(S, "aKT")
    VT = A.bf16(S, "aVT")
    ZG = A.bf16(S, "aZG")
    OA = A.f32(S, "aOA")
    DN = A.f32(S, "aDN")
    PT = [A.bf16(256, f"aPT{i}") for i in range(3)]
    PM = [A.bf16(256, f"aPM{i}") for i in range(3)]
    VK = [A.bf16(128, f"aVK{i}") for i in range(3)]
    YB = [A.bf16(512, f"aYB{i}") for i in range(2)]
    PS_S = [T(bank(k)[:, 0:256], pbuf(f"pss{k}")) for k in range(2)]
    PS_V = [T(bank(2 + k)[:, 0:64].bitcast(BF16), pbuf(f"psv{k}")) for k in range(2)]
    PS_O = [T(bank(4 + k)[:, 0:256], pbuf(f"pso{k}")) for k in range(2)]
    PS_D = [T(bank(6 + k)[:, 0:256], pbuf(f"psd{k}")) for k in range(2)]
    it = 0
    for h in range(2):
        P.dma("sp", QT, QKV[0 + h])
        P.dma("sp", KT, QKV[2 + h])
        P.dma("sp", VT, QKV[4 + h])
        P.dma("sp", ZG, ZS[2 + h])
        for q4 in range(4):
            P.memset("pool", OA[:, q4 * 2048:(q4 + 1) * 2048], 0.0)
            P.memset("pool", DN[:, q4 * 2048:(q4 + 1) * 2048], 0.0)
        for dil in (1, 4, 16):
            L = S // dil
            for r in range(dil):
                for kt in range(L // 128):
                    wlo = max(0, 128 * kt - 64)
                    whi = min(L, 128 * kt + 192)
                    nW = whi - wlo
                    off = wlo - (128 * kt - 64)
                    k0 = r + dil * 128 * kt
                    ksl = slice(k0, k0 + dil * 127 + 1, dil)
                    q0 = r + dil * wlo
                    qsl = slice(q0, q0 + dil * (nW - 1) + 1, dil)
                    pss = PS_S[it % 2]; psv = PS_V[it % 2]; pso = PS_O[it % 2]; psd = PS_D[it % 2]
                    pt = PT[it % 3]; pm = PM[it % 3]; vk = VK[it % 3]
                    it += 1
                    P.matmul(pss[:, 0:nW], KT[:, ksl], QT[:, qsl])
                    P.transpose(psv, VT[:, ksl], identb)
                    P.act(pt[:, 0:nW], pss[:, 0:nW], AF.Exp, scale=float(128 ** -0.5))
                    P.copy("act", vk, psv)
                    P.tt("pool", pm[:, 0:nW], pt[:, 0:nW], bandb[:, off:off + nW], ALU.mult)
                    P.matmul(pso[:, 0:nW], vk, pm[:, 0:nW])
                    P.matmul(psd[:, 0:nW], onesb, pm[:, 0:nW])
                    P.tt("dve", OA[:, qsl], pso[:, 0:nW], OA[:, qsl], ALU.add)
                    P.tt("dve", DN[:, qsl], psd[:, 0:nW], DN[:, qsl], ALU.add)
        for q in range(16):
            sl = slice(q * 512, (q + 1) * 512)
            yb = YB[q % 2]
            P.recip(DN[:, sl], DN[:, sl])
            P.tt("pool", OA[:, sl], OA[:, sl], DN[:, sl], ALU.mult)
            P.tt("dve", yb, OA[:, sl], ZG[:, sl], ALU.mult)
            P.dma("sp", yT[256 + h * 128:256 + (h + 1) * 128, sl], yb)


def build_gdn(nc, P, A, bank, env):
    U, ZS, yT, G = env["U"], env["ZS"], env["yT"], env["G"]
    cst, cstb = env["cst"], env["cstb"]
    ident32, identb, onesb = env["ident32"], env["identb"], env["onesb"]
    convw_s, alog_s, dtb_s, gnw_s, eps6 = env["convw_s"], env["alog_s"], env["dtb_s"], env["gnw_s"], env["eps6"]
    QT = A.bf16(S, "gQT")
    KT = A.bf16(S, "gKT")
    Ktok = v3(A.bf16(64 * 128, "gKtok"), 64)
    Vtok = v3(A.bf16(64 * 256, "gVtok"), 64)
    OtA = A.f32(64 * 256, "gOt")
    Ot = [T(OtA.ap[:, t * 256:(t + 1) * 256], Buf(f"Ot{t}")) for t in range(64)]
    work_mark = A.p
    mini = env.get("mini")
    if mini:
        return build_gdn_core(nc, P, A, bank, env, locals())

    UB = [A.f32(516, f"ub{i}") for i in range(2)]
    ACC = [A.f32(512, f"acc{i}") for i in range(2)]
    SB = [A.f32(512, f"sb{i}") for i in range(2)]
    SQ = [A.bf16(512, f"sq{i}") for i in range(2)]
    RS = [A.f32(512, f"rs{i}") for i in range(2)]
    PSN = [T(bank(k), pbuf(f"psn{k}")) for k in range(2)]
    PSTF = [T(bank(2 + k), pbuf(f"pstf{k}")) for k in range(2)]
    PSTB = [T(bank(4 + k)[:, 0:256].bitcast(BF16), pbuf(f"pstb{k}")) for k in range(2)]
    it = 0
    for ct in range(4):
        for tb in range(16):
            ub, acc, sb, sq, rs = UB[it % 2], ACC[it % 2], SB[it % 2], SQ[it % 2], RS[it % 2]
            eng = "dve"
            P.dma("sp", ub, U[ct][:, tb * 512:tb * 512 + 516])
            P.ts(eng, acc, ub[:, 0:512], convw_s[:, ct * 5:ct * 5 + 1], ALU.mult, 0.0, ALU.add)
            for k in range(1, 5):
                P.stt(eng, acc, ub[:, k:k + 512], convw_s[:, ct * 5 + k:ct * 5 + k + 1], acc, ALU.mult, ALU.add)
            P.act(sb, acc, AF.Silu)
            sl = slice(tb * 512, (tb + 1) * 512)
            if ct < 2:
                P.tt("pool", sq, sb, sb, ALU.mult)
                psn = PSN[it % 2]
                P.matmul(psn, onesb, sq)
                P.act(rs, psn, AF.Sqrt, bias=eps6)
                P.recip(rs, rs)
                dst = QT if ct == 0 else KT
                P.stt("dve", dst[:, sl], sb, float(128 ** -0.5) if ct == 0 else 1.0, rs, ALU.mult, ALU.mult)
                if ct == 1:
                    pst = PSTB[it % 2]
                    for j in range(4):
                        P.transpose(pst[:, j * 128:(j + 1) * 128], KT[:, tb * 512 + j * 128:tb * 512 + (j + 1) * 128], identb)
                    P.copy("act", Ktok[:, 4 * tb:4 * tb + 4, :], v3(pst, 4))
            else:
                h = ct - 2
                pst = PSTF[it % 2]
                for j in range(4):
                    P.transpose(pst[:, j * 128:(j + 1) * 128], sb[:, j * 128:(j + 1) * 128], ident32)
                P.copy("act", Vtok[:, 4 * tb:4 * tb + 4, h * 128:(h + 1) * 128], v3(pst, 4))
            it += 1
    P.barrier()
    A.p = work_mark
    if GDN_STOP == 0:
        return

    g4 = v3(A.f32(256, "g4"), 64)
    beta4 = v3(A.f32(256, "beta4"), 64)
    tmpg = v3(A.f32(256, "tmpg"), 64)
    nA = A.f32(4, "nA")
    G3 = v3(G, 64)
    P.act(nA, alog_s, AF.Exp)
    P.ts("dve", nA, nA, -1.0, ALU.mult, 0.0, ALU.add)
    for s_ in range(4):
        P.ts("dve", tmpg[:, :, s_], G3[:, :, s_], dtb_s[:, s_:s_ + 1], ALU.add, 0.0, ALU.add)
    P.act(tmpg, tmpg, AF.Exp)
    P.act(tmpg, tmpg, AF.Ln, bias=1.0)
    for s_ in range(4):
        P.ts("dve", g4[:, :, s_], tmpg[:, :, s_], nA[:, s_:s_ + 1], ALU.mult, 0.0, ALU.add)
    P.act(beta4, G3[:, :, 4:8], AF.Sigmoid)
    P.barrier()
    if GDN_STOP == 1:
        return
    return build_gdn_core(nc, P, A, bank, env, locals())


def build_gdn_core(nc, P, A, bank, env, loc):
    U, ZS, yT, G = env["U"], env["ZS"], env["yT"], env["G"]
    cst, cstb = env["cst"], env["cstb"]
    ident32, identb, onesb = env["ident32"], env["identb"], env["onesb"]
    gnw_s, eps6 = env["gnw_s"], env["eps6"]
    QT, KT, Ktok, Vtok, Ot, work_mark = (loc[k] for k in ("QT", "KT", "Ktok", "Vtok", "Ot", "work_mark"))
    env["Ot"] = Ot
    mini = env.get("mini")
    if mini:
        rnd = env["rnd"]
        g4 = v3(A.f32(256, "g4"), 64)
        beta4 = v3(A.f32(256, "beta4"), 64)
        rs_ = T(loc["OtA"].ap[:, 8192:12288], Buf("rnds"))
        P.dma("sp", rs_, rnd)
        P.copy("dve", QT[:, 0:256], rs_[:, 0:256])
        P.copy("dve", KT[:, 0:256], rs_[:, 256:512])
        P.copy("dve", Ktok[:, 0:2, :], v3(rs_[:, 512:768], 2))
        P.copy("dve", Vtok[:, 0:2, :], v3(rs_[:, 768:1280], 2))
        P.copy("dve", g4[:, 0:2, :], v3(rs_[:, 1280:1288], 2))
        P.copy("dve", beta4[:, 0:2, :], v3(rs_[:, 1288:1296], 2))
        P.barrier()
    else:
        g4, beta4 = loc["g4"], loc["beta4"]

    KQm = [A.f32(256, f"KQm{d}") for d in range(2)]
    Lg = [A.f32(128, f"Lg{s_}") for s_ in range(4)]
    E = [A.f32(128, f"E{s_}") for s_ in range(4)]
    STa = [[A.bf16(512, f"ST{s_}{k}") for k in range(2)] for s_ in range(4)]
    nidentb = env["nidentb"]
    AQ = [[A.bf16(256, f"AQ{p}{s_}") for s_ in range(4)] for p in range(2)]
    Y0 = [A.bf16(128, f"Y0{s_}") for s_ in range(4)]
    TTb = [[A.bf16(128, f"TTb{p}{s_}") for s_ in range(4)] for p in range(2)]
    ktb = [[A.bf16(128, f"ktb{p}{s_}") for s_ in range(4)] for p in range(2)]
    sc = [[A.f32(16, f"sc{p}{d}") for d in range(2)] for p in range(2)]
    gsel = [A.f32(4, f"gsel{d}") for d in range(2)]
    Sbf = [A.bf16(256, f"Sbf{d}") for d in range(2)]
    S32 = [A.f32(256, f"S32{d}") for d in range(2)]
    rb = [A.bf16(256, f"rb{d}") for d in range(2)]
    qs = [A.f32(256, f"qs{d}") for d in range(2)]
    vn = [A.bf16(256, f"vn{d}") for d in range(2)]
    tmpo = [A.f32(256, f"tmpo{d}") for d in range(2)]
    for d in range(2):
        P.memset("pool", gsel[d], 0.0)
        P.memset("pool", Sbf[d], 0.0)
        P.memset("pool", S32[d], 0.0)
    bb = [pbuf(f"gbank{k}") for k in range(8)]
    PKS = [T(bank(2 * d)[:, 0:256], bb[2 * d]) for d in range(2)]
    PSU = [T(bank(2 * d)[:, 256:512], bb[2 * d]) for d in range(2)]
    PQS = [T(bank(2 * d + 1)[:, 0:256], bb[2 * d + 1]) for d in range(2)]
    PVO = [T(bank(2 * d + 1)[:, 256:512], bb[2 * d + 1]) for d in range(2)]
    PK = T(bank(4)[:, 0:256], bb[4])
    PSM = [T(bank(4)[:, 256 + 8 * d:264 + 8 * d], bb[4]) for d in range(2)]
    PD = [T(bank(5)[:, 128 * h:128 * h + 128], bb[5]) for h in range(2)]
    PCB = [T(bank(5)[:, 256 + 64 * h:320 + 64 * h].bitcast(BF16), bb[5]) for h in range(2)]
    PCH = [T(bank(6 + h), bb[6 + h]) for h in range(2)]
    mLT, mLE, mGT, mGE = (cst[:, o:o + 128] for o in (C_LT, C_LE, C_GT, C_GE))
    mBO, mONE = cst[:, C_BO:C_BO + 128], cst[:, C_ONE:C_ONE + 128]
    MstrictT = [mLT, mGT]
    Mincl = [mGE, mLE]
    Mcat = [cst[:, C_GT:C_GT + 256], cst[:, C_LT:C_LT + 256]]

    def g1(tile, d, par):
        kt = KT[:, tile * 128:(tile + 1) * 128]
        qt = QT[:, tile * 128:(tile + 1) * 128]
        scd = sc[par][d]
        gcol = g4[:, tile, 2 * d:2 * d + 2]
        bcol = beta4[:, tile, 2 * d:2 * d + 2]
        P.matmul(PK[:, 0:128], kt, kt)
        P.matmul(PK[:, 128:256], kt, qt)
        P.tt("dve", KQm[d], PK, Mcat[d], ALU.mult)
        yield
        psm = PSM[d]
        P.matmul(psm[:, 0:2], MstrictT[d], gcol)
        P.matmul(psm[:, 2:4], mBO, gcol)
        gs3 = v3(gsel[d], 2)
        P.copy("pool", gs3[0:64, :, 0], gcol[0:64, :])
        P.copy("pool", gs3[64:128, :, 1], gcol[64:128, :])
        P.matmul(psm[:, 4:8], mONE, gsel[d])
        P.copy("act", scd[:, 0:2], psm[:, 0:2])
        P.tt("dve", scd[:, 2:4], psm[:, 2:4], scd[:, 0:2], ALU.subtract)
        P.act(scd[:, 4:6], scd[:, 2:4], AF.Exp)
        P.ts("dve", scd[:, 6:8], scd[:, 4:6], -1.0, ALU.mult, 0.0, ALU.add)
        P.act(scd[:, 8:10], scd[:, 0:2], AF.Exp)
        P.tt("dve", scd[:, 8:10], scd[:, 8:10], bcol, ALU.mult)
        P.act(scd[:, 10:14], psm[:, 4:8], AF.Exp)
        yield
        for h in range(2):
            s_ = 2 * d + h
            b1 = beta4[:, tile, s_:s_ + 1]
            P.ts("pool", Lg[s_], Mincl[d], g4[:, tile, s_:s_ + 1], ALU.mult, 0.0, ALU.add)
            P.matmul(PD[d], MstrictT[d], Lg[s_])
            P.act(E[s_], PD[d], AF.Exp)
            aq = AQ[par][s_]
            P.stt("dve", v3(aq, 2), T(E[s_].ap.unsqueeze(1).to_broadcast([128, 2, 128]), E[s_].buf), b1,
                  v3(KQm[d], 2), ALU.mult, ALU.mult)
            P.ts("pool", ktb[par][s_], Ktok[:, tile, :], scd[:, 8 + h:9 + h], ALU.mult, 0.0, ALU.add)
            yield
            X0 = aq[:, 0:128]
            pcb = PCB[d]
            pch = PCH[d]
            ev = "act" if d == 0 else "dve"
            bc2 = lambda t_: T(t_.ap.unsqueeze(1).to_broadcast([128, 2, 128]), t_.buf)
            P.transpose(pcb, X0, identb)
            P.copy("act", Y0[s_], pcb)
            P.matmul(pch[:, 0:128], Y0[s_], X0)
            P.matmul(pch[:, 128:256], identb, identb, start=True, stop=False)
            P.matmul(pch[:, 128:256], nidentb, X0, start=False, stop=True)
            P.matmul(v3(pch[:, 256:512], 2), X0, bc2(Y0[s_]), start=True, stop=False)
            P.matmul(pch[:, 384:512], identb, identb, start=False, stop=True)
            P.copy(ev, STa[s_][1], pch)
            yield
            for k in range(1, 5):
                cur = STa[s_][k % 2]
                nxt = STa[s_][(k + 1) % 2]
                P.matmul(pch[:, 0:128], cur[:, 256:384], cur[:, 0:128])
                P.matmul(pch[:, 128:256], cur[:, 384:512], cur[:, 128:256])
                P.matmul(v3(pch[:, 256:512], 2), cur[:, 0:128], bc2(cur[:, 256:384]), start=True, stop=False)
                P.matmul(pch[:, 384:512], identb, identb, start=False, stop=True)
                P.copy(ev, nxt, pch)
                yield
            cur = STa[s_][1]
            P.matmul(pch[:, 128:256], cur[:, 384:512], cur[:, 128:256])
            P.copy(ev, TTb[par][s_], pch[:, 128:256])
            yield

    def g2(tile, d, par, first):
        kt = KT[:, tile * 128:(tile + 1) * 128]
        qt = QT[:, tile * 128:(tile + 1) * 128]
        scd = sc[par][d]
        for c in ((0, 1) if d == 0 else (1, 0)):
            R = slice(64 * c, 64 * c + 64)
            P.matmul(PKS[d], kt, Sbf[d])
            P.matmul(PQS[d], qt, Sbf[d])
            for h in range(2):
                hs = slice(h * 128, (h + 1) * 128)
                P.stt("dve", rb[d][R, hs], PKS[d][R, hs], scd[R, 6 + h:7 + h], Vtok[R, tile, hs], ALU.mult, ALU.add)
            for h in range(2):
                hs = slice(h * 128, (h + 1) * 128)
                P.act(qs[d][R, hs], PQS[d][R, hs], AF.Identity, scale=scd[R, 4 + h:5 + h])
            yield
            for h in range(2):
                hs = slice(h * 128, (h + 1) * 128)
                P.matmul(PVO[d][:, hs], TTb[par][2 * d + h][R, :], rb[d][R, hs])
            P.copy("act", vn[d][R, :], PVO[d][R, :])
            yield
            for h in range(2):
                hs = slice(h * 128, (h + 1) * 128)
                P.matmul(PSU[d][:, hs], ktb[par][2 * d + h][R, :], vn[d][R, hs])
            for h in range(2):
                hs = slice(h * 128, (h + 1) * 128)
                P.matmul(PVO[d][:, hs], AQ[par][2 * d + h][R, 128:256], vn[d][R, hs])
            for h in range(2):
                hs = slice(h * 128, (h + 1) * 128)
                eg = scd[:, 10 + 2 * h + c:11 + 2 * h + c]
                P.stt("dve", Sbf[d][:, hs], S32[d][:, hs], eg, PSU[d][:, hs], ALU.mult, ALU.add)
            for h in range(2):
                hs = slice(h * 128, (h + 1) * 128)
                eg = scd[:, 10 + 2 * h + c:11 + 2 * h + c]
                P.stt("dve", S32[d][:, hs], S32[d][:, hs], eg, PSU[d][:, hs], ALU.mult, ALU.add)
            if first:
                P.tt("dve", Ot[tile][R, :], PVO[d][R, :], qs[d][R, :], ALU.add)
            else:
                P.tt("dve", tmpo[d][R, :], PVO[d][R, :], qs[d][R, :], ALU.add)
                P.tt("pool", Ot[tile][R, :], Ot[tile][R, :], tmpo[d][R, :], ALU.add)
            yield

    def run_round_robin(gens):
        gens = list(gens)
        while gens:
            for g_ in list(gens):
                try:
                    next(g_)
                except StopIteration:
                    gens.remove(g_)

    NT = 64 if not mini else 2
    if GDN_STOP == 2:
        import itertools
        n1 = int(os.environ.get('G1_STOP', '99'))
        for g_ in (g1(0, 0, 0), g1(NT - 1, 1, 0)):
            for _ in itertools.islice(g_, n1):
                pass
        return
    run_round_robin([g1(0, 0, 0), g1(NT - 1, 1, 0)])
    for i in range(NT):
        par = i % 2
        gens = [g2(i, 0, par, i < NT // 2), g2(NT - 1 - i, 1, par, i < NT // 2)]
        if i + 1 < NT:
            gens += [g1(i + 1, 0, 1 - par), g1(NT - 2 - i, 1, 1 - par)]
        run_round_robin(gens)
    P.barrier()
    A.p = work_mark
    if GDN_STOP == 3 or mini:
        return

    sqb = [A.f32(256, f"g3sq{i}") for i in range(2)]
    ssb = [A.f32(2, f"g3ss{i}") for i in range(2)]
    onb = [A.f32(256, f"g3on{i}") for i in range(2)]
    zg = [v3(A.bf16(1024, f"g3zg{i}"), 2) for i in range(2)]
    yb = [v3(A.bf16(256, f"g3yb{i}"), 2) for i in range(2)]
    PT3 = [T(bank(k)[:, 0:256], pbuf(f"pt3{k}")) for k in range(2)]
    yTv = yT[0:256, :].re("(h p) t -> p h t", h=2)
    ZSv = T(ZS.ap[0:2].rearrange("h p t -> p h t"), ZS.buf)
    for tile in range(NT):
        i2 = tile % 2
        if tile % 4 == 0:
            zt_ = zg[(tile // 4) % 2]
            P.dma("sp", zt_, ZSv[:, :, tile * 128:tile * 128 + 512])
        zt_ = zg[(tile // 4) % 2]
        o = Ot[tile]
        P.tt("pool", sqb[i2], o, o, ALU.mult)
        P.rsum(ssb[i2], v3(sqb[i2], 2))
        P.act(ssb[i2], ssb[i2], AF.Sqrt, bias=eps6, scale=1.0 / 128.0)
        P.recip(ssb[i2], ssb[i2])
        for h in range(2):
            hs = slice(h * 128, (h + 1) * 128)
            P.stt("dve", onb[i2][:, hs], o[:, hs], ssb[i2][:, h:h + 1], gnw_s, ALU.mult, ALU.mult)
            P.transpose(PT3[i2][:, hs], onb[i2][:, hs], ident32)
        P.tt("dve", yb[i2], v3(PT3[i2], 2), zt_[:, :, (tile % 4) * 128:(tile % 4) * 128 + 128], ALU.mult)
        P.dma("sp", yTv[:, :, tile * 128:(tile + 1) * 128], yb[i2])


def core_cols(c):
    o = OFFS
    cols = np.concatenate([
        np.arange(o[0] + 128 * c, o[0] + 128 * c + 128),
        np.arange(o[1] + 128 * c, o[1] + 128 * c + 128),
        np.arange(o[2] + 256 * c, o[2] + 256 * c + 256),
        np.arange(o[3] + 256 * c, o[3] + 256 * c + 256),
        np.arange(o[6] + 256 * c, o[6] + 256 * c + 256),
        np.arange(o[7] + 256 * c, o[7] + 256 * c + 256),
        np.arange(o[8] + 256 * c, o[8] + 256 * c + 256),
        np.arange(o[9] + 256 * c, o[9] + 256 * c + 256)])
    gc = np.array([o[4] + 2 * c, o[4] + 2 * c + 1, o[4] + 16 + 2 * c, o[4] + 16 + 2 * c + 1,
                   o[5] + 2 * c, o[5] + 2 * c + 1, o[5] + 16 + 2 * c, o[5] + 16 + 2 * c + 1])
    return cols, gc


def prep_A(inputs):
    x = np.asarray(inputs["x"], np.float32)[0]
    w_in = np.asarray(inputs["w_in"], np.float32)[0]
    conv_w = np.asarray(inputs["conv_w"], np.float32)[0]
    a_log = np.asarray(inputs["a_log"], np.float32)[0]
    dt_bias = np.asarray(inputs["dt_bias"], np.float32)[0]
    gnw = np.asarray(inputs["gdn_norm_w"], np.float32)[0]
    xT = np.ascontiguousarray(x.T)
    cosT, sinX = rope_tables()
    cst = make_consts()
    maps = []
    for c in range(NCORES):
        cols, gcols = core_cols(c)
        ch = np.concatenate([np.arange(128 * c, 128 * c + 128), 1024 + np.arange(128 * c, 128 * c + 128),
                             2048 + np.arange(256 * c, 256 * c + 256)])
        cw = conv_w[:, ch].reshape(5, 4, 128).transpose(2, 1, 0).reshape(128, 20)
        st = [(0, 2 * c), (0, 2 * c + 1), (1, 2 * c), (1, 2 * c + 1)]
        al = np.array([a_log[d, h] for d, h in st], np.float32)
        db = np.array([dt_bias[d, h] for d, h in st], np.float32)
        maps.append({
            "xT": xT,
            "w1": np.ascontiguousarray(w_in[:, cols]),
            "wg": np.ascontiguousarray(w_in[:, gcols]),
            "convw": np.ascontiguousarray(cw),
            "alog": np.ascontiguousarray(np.broadcast_to(al, (128, 4))),
            "dtb": np.ascontiguousarray(np.broadcast_to(db, (128, 4))),
            "gnw": np.ascontiguousarray(np.broadcast_to(gnw, (128, 128))),
            "cosT": cosT, "sinX": sinX, "cst": cst,
        })
    return maps


def run_A(inputs, stages=("gdn", "att")):
    nc = build_A(stages)
    res = run_bass_kernel_spmd(nc, prep_A(inputs), core_ids=list(range(NCORES)))
    return [np.asarray(r["yT"]) for r in res.results]


def build_mini():
    nc = bass.Bass("TRN2", target_bir_lowering=False)
    rnd = T(nc.dram_tensor("rnd", [128, 4096], F32, kind="ExternalInput").ap())
    cst_d = T(nc.dram_tensor("cst", [128, C_N], F32, kind="ExternalInput").ap())
    ot = T(nc.dram_tensor("ot", [128, 512], F32, kind="ExternalOutput").ap())
    P = Prog(nc)
    with contextlib.ExitStack() as st:
        arena_t = st.enter_context(nc.sbuf_tensor("arena", [128, ARENA], F32))
        ps_t = st.enter_context(nc.psum_tensor("ps", [128, 4096], F32))
        A = Arena(arena_t, ARENA)
        bank = lambda k: ps_t[:, 512 * k:512 * (k + 1)]
        cst = A.f32(C_N, "cst")
        cstb = A.bf16(C_N, "cstb")
        small = A.f32(160, "small")
        P.dma("sp", cst, cst_d)
        P.copy("dve", cstb, cst)
        nidentb = A.bf16(128, "nidentb")
        P.ts("dve", nidentb, cstb[:, C_ID:C_ID + 128], -1.0, ALU.mult, 0.0, ALU.add)
        env = dict(nidentb=nidentb, U=None, ZS=None, yT=None, G=None, cst=cst, cstb=cstb, ident32=cst[:, C_ID:C_ID + 128],
                   identb=cstb[:, C_ID:C_ID + 128], onesb=cstb[:, C_ONE:C_ONE + 128], convw_s=None, alog_s=None,
                   dtb_s=None, gnw_s=None, eps6=None, mini=True, rnd=rnd)
        loc = build_gdn(nc, P, A, bank, env)
        P.barrier()
        if GDN_STOP != 2:
            for t in range(2):
                P.dma("sp", ot[:, t * 256:(t + 1) * 256], env["Ot"][t])
        P.emit()
    return nc


ALPHA = float(2.0 ** 0.25)


def emit_B(nc, P, A, bank, yTa, x_c, pT_c, wo, wgt, wup, lng, out_c, dyn=False):
    NB = 512
    NNB = D // NB
    YT = v3(A.bf16(32 * 512, "bYT"), 32)
    H = v3(A.f32(4 * D, "bH"), 4)
    WB = [v3(A.bf16(32 * NB, f"bWB{i}"), 32) for i in range(2)]
    WU = [v3(A.bf16(2 * NB, f"bWU{i}"), 2) for i in range(2)]
    PTb = v3(A.bf16(2 * 512, "bPT"), 2)
    XS = [v3(A.f32(4 * NB, f"bXS{i}"), 4) for i in range(1)]
    GB = [[A.f32(512, f"bGB{i}{j}") for j in range(2)] for i in range(1)]
    SG = [A.f32(NB, f"bSG{i}") for i in range(2)]
    PL = [A.f32(NB, f"bPL{i}") for i in range(2)]
    junk = A.bf16(512, "bjunk")
    stat = A.f32(64, "bstat")
    epsln = A.f32(1, "bepsln")
    ident32 = A.f32(128, "bident")
    P.memset("dve", epsln, 1e-5)
    P.memset("pool", ident32, 0.0)
    P.dma("sp", ident32, lng["ident"])
    PSB = [T(bank(k)[:, 0:NB], pbuf(f"psb{k}")) for k in range(8)]
    assert NB <= 512
    PSX = [T(bank(k), pbuf(f"psx{k}")) for k in range(8)]
    yv = yTa.re("(kc p) t -> p kc t", p=128)
    wov = wo.re("(kc p) n -> p kc n", p=128)
    wgv = wgt.re("(kc p) n -> p kc n", p=128)
    wuv = wup.re("(kc p) n -> p kc n", p=128)
    pv = pT_c.re("(kc p) t -> p kc t", p=128)
    xv = x_c.re("(tt p) n -> p tt n", p=128)
    ov = out_c.re("(tt p) n -> p tt n", p=128)
    cnt = [0]

    def layer_norm(gname, bname):
        for tt in range(4):
            h = H[:, tt, :]
            st_ = stat[:, tt * 16:(tt + 1) * 16]
            P.rsum(st_[:, 0:1], h)
            for j in range(8):
                P.add("act", lambda e, j=j, h=h, st_=st_: e.activation(junk.ap, h.ap[:, j * 512:(j + 1) * 512], AF.Square,
                                                                     accum_out=st_.ap[:, 2 + j:3 + j]),
                      reads=[h], writes=[junk, st_])
            P.rsum(st_[:, 1:2], st_[:, 2:10])
            P.ts("dve", st_[:, 10:11], st_[:, 0:1], 1.0 / D, ALU.mult, 0.0, ALU.add)
            P.ts("dve", st_[:, 11:12], st_[:, 1:2], 1.0 / D, ALU.mult, 0.0, ALU.add)
            P.tt("dve", st_[:, 12:13], st_[:, 10:11], st_[:, 10:11], ALU.mult)
            P.tt("dve", st_[:, 11:12], st_[:, 11:12], st_[:, 12:13], ALU.subtract)
            P.act(st_[:, 13:14], st_[:, 11:12], AF.Sqrt, bias=epsln)
            P.recip(st_[:, 13:14], st_[:, 13:14])
            P.ts("dve", h, h, st_[:, 10:11], ALU.subtract, st_[:, 13:14], ALU.mult)
        for blk in range(8):
            gb = GB[0]
            sl = slice(blk * 512, (blk + 1) * 512)
            P.dma("sp", gb[0], lng[gname][:, sl])
            P.dma("sp", gb[1], lng[bname][:, sl])
            for tt in range(4):
                P.tt("pool", H[:, tt, sl], H[:, tt, sl], gb[0], ALU.mult)
                P.tt("dve", H[:, tt, sl], H[:, tt, sl], gb[1], ALU.add)

    for hp in range(2):
        t0 = hp * 512
        if dyn:
            P.add("pool", lambda e, sem, t0=t0: e.dma_start(
                out=YT.ap, in_=yv.ap[:, :, bass.ds(e.partition_id() * 1024 + t0, 512)]).then_inc(sem, 16),
                reads=[yTa], writes=[YT], dma=1)
        else:
            P.dma("sp", YT, yv[:, :, t0:t0 + 512])
        for nb in range(NNB):
            wb = WB[cnt[0] % 2]; xs = XS[0]; cnt[0] += 1
            nsl = slice(nb * NB, (nb + 1) * NB)
            P.dma("pool", wb, wov[:, :, nsl])
            P.dma("sp", xs, xv[:, 4 * hp:4 * hp + 4, nsl])
            for tt in range(4):
                ps = PSB[(nb * 4 + tt) % 8]
                for kc in range(32):
                    P.matmul(ps, YT[:, kc, tt * 128:(tt + 1) * 128], wb[:, kc, :], start=(kc == 0), stop=(kc == 31))
                P.stt("dve", H[:, tt, nsl], xs[:, tt, :], ALPHA, ps, ALU.mult, ALU.add)
        layer_norm("ln1g", "ln1b")
        for kc in range(32):
            px = PSX[kc % 8]
            for tt in range(4):
                P.transpose(px[:, tt * 128:(tt + 1) * 128], H[:, tt, kc * 128:(kc + 1) * 128], ident32)
            P.copy("act" if kc % 2 == 0 else "dve", YT[:, kc, :], px)
        P.dma("pool", PTb, pv[:, :, t0:t0 + 512])
        for nb in range(NNB):
            wb = WB[cnt[0] % 2]; wu = WU[cnt[0] % 2]; cnt[0] += 1
            nsl = slice(nb * NB, (nb + 1) * NB)
            P.dma("pool", wb, wgv[:, :, nsl])
            P.dma("pool", wu, wuv[:, :, nsl])
            for tt in range(4):
                pg = PSB[tt]
                pu = PSB[4 + tt]
                tsl = slice(tt * 128, (tt + 1) * 128)
                for kc in range(32):
                    P.matmul(pg, YT[:, kc, tsl], wb[:, kc, :], start=(kc == 0), stop=(kc == 31))
                for kc in range(2):
                    P.matmul(pu, PTb[:, kc, tsl], wu[:, kc, :], start=(kc == 0), stop=(kc == 1))
                sg = SG[tt % 2]; pl = PL[tt % 2]
                P.act(sg, pg, AF.Sigmoid)
                P.tt("dve", pl, pu, sg, ALU.mult)
                P.stt("dve", H[:, tt, nsl], H[:, tt, nsl], ALPHA, pl, ALU.mult, ALU.add)
        layer_norm("ln2g", "ln2b")
        for tt in range(4):
            P.dma("sp", ov[:, 4 * hp + tt, :], H[:, tt, :])


def build_B():
    nc = bass.Bass("TRN2", target_bir_lowering=False)
    din = lambda n, s, dt=F32: T(nc.dram_tensor(n, s, dt, kind="ExternalInput").ap(), Buf(n))
    yTa = din("yTa", [D, 1024], BF16)
    x_c = din("x_c", [1024, D])
    pT_c = din("pT_c", [256, 1024])
    wo = din("wo", [D, D])
    wgt = din("wgt", [D, D])
    wup = din("wup", [256, D])
    lng = {k: din(k, [128, D]) for k in ("ln1g", "ln1b", "ln2g", "ln2b")}
    lng["ident"] = din("ident", [128, 128])
    out_c = T(nc.dram_tensor("out_c", [1024, D], F32, kind="ExternalOutput").ap(), Buf("out_c"))
    P = Prog(nc)
    with contextlib.ExitStack() as st:
        arena_t = st.enter_context(nc.sbuf_tensor("arena", [128, ARENA], F32))
        ps_t = st.enter_context(nc.psum_tensor("ps", [128, 4096], F32))
        A = Arena(arena_t, ARENA)
        bank = lambda k: ps_t[:, 512 * k:512 * (k + 1)]
        emit_B(nc, P, A, bank, yTa, x_c, pT_c, wo, wgt, wup, lng, out_c)
        P.emit()
    return nc


def prep_B(inputs, yTs):
    x = np.asarray(inputs["x"], np.float32)[0]
    p = np.asarray(inputs["p"], np.float32)[0, 0]
    w_out = np.asarray(inputs["w_out"], np.float32)[0]
    wgt = np.ascontiguousarray(np.asarray(inputs["w_ple_gate"], np.float32)[0])
    wup = np.ascontiguousarray(np.asarray(inputs["w_ple_up"], np.float32)[0])
    rows = np.concatenate([np.concatenate([np.arange(256 * r, 256 * r + 256), 2048 + np.arange(256 * r, 256 * r + 256)])
                           for r in range(NCORES)])
    wo = np.ascontiguousarray(w_out[rows])
    yall = np.concatenate(yTs, 0) if yTs is not None else None
    bc = lambda k: np.ascontiguousarray(np.broadcast_to(np.asarray(inputs[k], np.float32)[0], (128, D)))
    lnp = {"ln1g": bc("ln1_g"), "ln1b": bc("ln1_b"), "ln2g": bc("ln2_g"), "ln2b": bc("ln2_b")}
    ident = np.eye(128, dtype=np.float32)
    pT = np.ascontiguousarray(p.T)
    maps = []
    for c in range(NCORES):
        sl = slice(1024 * c, 1024 * (c + 1))
        m = {"x_c": np.ascontiguousarray(x[sl]), "pT_c": np.ascontiguousarray(pT[:, sl]), "wo": wo, "wgt": wgt,
             "wup": wup, "ident": ident, **lnp}
        if yall is not None:
            m["yTa"] = np.ascontiguousarray(yall[:, sl])
        maps.append(m)
    return maps


def kernel_unfused(**inputs):
    yTs = run_A(inputs)
    ncB = build_B()
    res = run_bass_kernel_spmd(ncB, prep_B(inputs, yTs), core_ids=list(range(NCORES)))
    out = np.concatenate([np.asarray(r["out_c"]) for r in res.results], 0)
    return out.reshape(1, S, D).astype(np.float32)


def kernel(**inputs):
    nc = build_A(fused=True)
    mA = prep_A(inputs)
    mB = prep_B(inputs, None)
    maps = [{**a, **b} for a, b in zip(mA, mB)]
    res = run_bass_kernel_spmd(nc, maps, core_ids=list(range(NCORES)))
    out = np.concatenate([np.asarray(r["out_c"]) for r in res.results], 0)
    return out.reshape(1, S, D).astype(np.float32)
```

```python
import contextlib
import os
import numpy as np
import ml_dtypes
import concourse.bass as bass
import concourse.mybir as mybir
from concourse.bass_utils import run_bass_kernel_spmd

F32 = mybir.dt.float32
BF16 = mybir.dt.bfloat16
AF = mybir.ActivationFunctionType
ALU = mybir.AluOpType
AX = mybir.AxisListType

NCORES = 8
S = 8192
D = 4096
SPLITS = (1024, 1024, 2048, 2048, 32, 32, 2048, 2048, 2048, 2048)
OFFS = np.concatenate([[0], np.cumsum(SPLITS)]).astype(int)
ARENA = 48 * 1024 - 512
import os
GDN_STOP = int(os.environ.get('GDN_STOP', '9'))
FAST = int(os.environ.get('FAST', '0'))


class Buf:
    __slots__ = ("name", "last_w", "readers", "excl")

    def __init__(self, name="", excl=False):
        self.name = name
        self.last_w = None
        self.readers = []
        self.excl = excl


def pbuf(name=""):
    return Buf(name, excl=not int(os.environ.get("NOEXCL", "0")))


class T:
    __slots__ = ("ap", "buf")

    def __init__(self, ap, buf=None):
        self.ap = ap
        self.buf = buf if buf is not None else Buf()

    def __getitem__(self, k):
        return T(self.ap[k], self.buf)

    def re(self, s, **kw):
        return T(self.ap.rearrange(s, **kw), self.buf)

    def bc(self, dt):
        return T(self.ap.bitcast(dt), self.buf)

    def on(self, buf):
        return T(self.ap, buf)


class Op:
    __slots__ = ("eng", "fn", "deps", "dma", "signal", "sem", "val", "idx")


class Prog:
    ENGS = ("pe", "act", "dve", "pool", "sp")
    LIMIT = 30000
    NDS = 24

    def __init__(self, nc):
        self.nc = nc
        self.ops = []
        self.last_dma_on_sem = [None] * self.NDS
        self.ndma = 0
        self.ndma_sw = 0
        self.pending_dma = []
        self.last_op = {e: None for e in self.ENGS}

    def add(self, eng, fn, reads=(), writes=(), dma=0, extra_deps=()):
        idx = len(self.ops)
        reads = [t.buf for t in reads if isinstance(t, T)]
        writes = [t.buf for t in writes if isinstance(t, T)]
        deps = set(extra_deps)
        raw = set()
        for b in reads:
            if b.last_w is not None:
                deps.add(b.last_w)
                raw.add(b.last_w)
            if b.excl:
                deps.update(r for r in b.readers if self.ops[r].eng != eng)
        for b in writes:
            if b.last_w is not None:
                deps.add(b.last_w)
            deps.update(b.readers)
        keep = set(extra_deps)
        for d in deps:
            p = self.ops[d]
            if p.dma or dma:
                keep.add(d)
                continue
            if p.eng == eng and eng == "pe":
                continue
            keep.add(d)
        op = Op()
        op.eng, op.fn, op.dma, op.idx = eng, fn, dma, idx
        op.signal = False
        op.sem = op.val = None
        if dma:
            if eng == "pool":
                slot = self.ndma_sw % 8
                self.ndma_sw += 1
            else:
                slot = 8 + self.ndma % (self.NDS - 8)
                self.ndma += 1
            prev = self.last_dma_on_sem[slot]
            if prev is not None:
                keep.add(prev)
            self.last_dma_on_sem[slot] = idx
            op.sem = slot
            self.pending_dma.append(idx)
        op.deps = sorted(keep)
        for d in op.deps:
            self.ops[d].signal = True
        self.ops.append(op)
        for b in reads:
            b.readers.append(idx)
        for b in writes:
            b.last_w = idx
            b.readers = []
        self.last_op[eng] = idx
        return idx

    def barrier(self):
        deps = [v for v in self.last_op.values() if v is not None] + list(self.pending_dma)
        self.pending_dma = []
        for e in self.ENGS:
            self.add(e, None, extra_deps=deps)

    @staticmethod
    def _a(x):
        return x.ap if isinstance(x, T) else x

    def matmul(self, out, lhsT, rhs, start=True, stop=True, xr=()):
        a = self._a
        return self.add("pe", lambda e: e.matmul(a(out), a(lhsT), a(rhs), start=start, stop=stop),
                        reads=[lhsT, rhs, *xr], writes=[out])

    def transpose(self, out, in_, ident):
        a = self._a
        return self.add("pe", lambda e: e.transpose(a(out), a(in_), a(ident)),
                        reads=[in_, ident], writes=[out])

    def act(self, out, in_, func, bias=None, scale=None, xw=()):
        a = self._a
        kw = {}
        if bias is not None:
            kw["bias"] = a(bias)
        if scale is not None:
            kw["scale"] = a(scale)
        return self.add("act", lambda e: e.activation(a(out), a(in_), func, **kw),
                        reads=[in_, bias, scale], writes=[out, *xw])

    def tt(self, eng, out, in0, in1, op, xr=(), xw=()):
        a = self._a
        return self.add(eng, lambda e: e.tensor_tensor(a(out), a(in0), a(in1), op),
                        reads=[in0, in1, *xr], writes=[out, *xw])

    def ts(self, eng, out, in0, s1, op0, s2=None, op1=None):
        a = self._a
        if op1 is None:
            return self.add(eng, lambda e: e.tensor_scalar(a(out), a(in0), a(s1), None, op0),
                            reads=[in0, s1], writes=[out])
        return self.add(eng, lambda e: e.tensor_scalar(a(out), a(in0), a(s1), a(s2), op0, op1),
                        reads=[in0, s1, s2], writes=[out])

    def stt(self, eng, out, in0, scalar, in1, op0, op1, xr=()):
        a = self._a
        return self.add(eng, lambda e: e.scalar_tensor_tensor(a(out), a(in0), a(scalar), a(in1), op0, op1),
                        reads=[in0, scalar, in1, *xr], writes=[out])

    def copy(self, eng, out, in_, xr=(), xw=()):
        a = self._a
        if eng == "act":
            return self.add(eng, lambda e: e.copy(a(out), a(in_)), reads=[in_, *xr], writes=[out, *xw])
        return self.add(eng, lambda e: e.tensor_copy(a(out), a(in_)), reads=[in_, *xr], writes=[out, *xw])

    def memset(self, eng, out, val):
        a = self._a
        return self.add(eng, lambda e: e.memset(a(out), val), writes=[out])

    def recip(self, out, in_):
        a = self._a
        return self.add("dve", lambda e: e.reciprocal(a(out), a(in_)), reads=[in_], writes=[out])

    def rsum(self, out, in_):
        a = self._a
        return self.add("dve", lambda e: e.reduce_sum(a(out), a(in_), AX.X), reads=[in_], writes=[out])

    def collective(self, fn, reads, writes):
        idx = self.add("pool", fn, reads=reads, writes=writes, dma=1)
        self.ops[idx].sem = "cc"
        return idx

    def dma(self, eng, out, in_, **kw):
        a = self._a
        return self.add(eng, lambda e, sem: e.dma_start(out=a(out), in_=a(in_), **kw).then_inc(sem, 16),
                        reads=[in_], writes=[out], dma=1)

    def emit(self):
        nc = self.nc
        final_wait_ops = [i for i, o in enumerate(self.ops) if o.dma]
        cnt = {e: 0 for e in self.ENGS}
        semidx = {e: 0 for e in self.ENGS}
        dcount = {i: 0 for i in range(self.NDS)}
        dcount["cc"] = 0
        nsem = {e: 1 for e in self.ENGS}
        for op in self.ops:
            if op.dma:
                dcount[op.sem] += 1 if op.sem == "cc" else 16 * op.dma
                op.val = dcount[op.sem]
                op.sem = ("d", op.sem)
            elif op.signal and op.fn is not None:
                if cnt[op.eng] >= self.LIMIT:
                    cnt[op.eng] = 0
                    semidx[op.eng] += 1
                    nsem[op.eng] = semidx[op.eng] + 1
                cnt[op.eng] += 1
                op.sem = (op.eng, semidx[op.eng])
                op.val = cnt[op.eng]
        with contextlib.ExitStack() as st:
            sems = {}
            for e in self.ENGS:
                for i in range(nsem[e]):
                    sems[(e, i)] = st.enter_context(nc.semaphore(f"s_{e}{i}"))
            for i in range(self.NDS):
                sems[("d", i)] = st.enter_context(nc.semaphore(f"s_d{i}"))
            sems[("d", "cc")] = st.enter_context(nc.semaphore("s_cc"))
            block = st.enter_context(nc.Block())
            engfun = {"pe": block.tensor, "act": block.scalar, "dve": block.vector,
                      "pool": block.gpsimd, "sp": block.sync}
            ops = self.ops

            def make(ename):
                def body(eng):
                    waited = {}

                    def wait_for(d):
                        p = ops[d]
                        if p.sem is None:
                            return
                        if waited.get(p.sem, 0) < p.val:
                            eng.wait_ge(sems[p.sem], p.val)
                            waited[p.sem] = p.val

                    for o in ops:
                        if o.eng != ename:
                            continue
                        for d in o.deps:
                            wait_for(d)
                        if o.fn is None:
                            continue
                        if o.dma:
                            o.fn(eng, sems[o.sem])
                        else:
                            ins = o.fn(eng)
                            if o.signal:
                                ins.then_inc(sems[o.sem], 1)
                    if ename == "sp":
                        for d in final_wait_ops:
                            wait_for(d)
                return body

            for e in self.ENGS:
                engfun[e](make(e))


class Arena:
    def __init__(self, tensor, ncols):
        self.t = tensor
        self.n = ncols
        self.p = 0

    def f32(self, n, name=""):
        off = self.p
        self.p += n
        assert self.p <= self.n, (self.p, self.n, name)
        return T(self.t[:, off:off + n], Buf(name))

    def bf16(self, n, name=""):
        nf = (n + 1) // 2
        off = self.p
        self.p += nf
        assert self.p <= self.n, (self.p, self.n, name)
        return T(self.t[:, off:off + nf].bitcast(BF16), Buf(name))


def v3(t, a):
    return t.re("p (a b) -> p a b", a=a)


C_ID, C_LT, C_LE, C_GT, C_GE, C_BO, C_ONE, C_BAND, C_N = 0, 128, 256, 384, 512, 640, 768, 896, 1152


def make_consts():
    r = np.arange(128)[:, None]
    c = np.arange(128)[None, :]
    same = (r // 64) == (c // 64)
    cst = np.zeros((128, C_N), np.float32)
    cst[:, C_ID:C_ID + 128] = (r == c)
    cst[:, C_LT:C_LT + 128] = same & (r > c)
    cst[:, C_LE:C_LE + 128] = same & (r >= c)
    cst[:, C_GT:C_GT + 128] = same & (r < c)
    cst[:, C_GE:C_GE + 128] = same & (r <= c)
    cst[:, C_BO:C_BO + 128] = same
    cst[:, C_ONE:C_ONE + 128] = 1.0
    w = np.arange(256)[None, :]
    cst[:, C_BAND:C_BAND + 256] = (w >= r) & (w <= r + 128)
    return cst


def rope_tables():
    half = 64
    inv_freq = (1.0 / (np.float32(10000.0) ** (np.arange(half, dtype=np.float32) / np.float32(half)))).astype(np.float32)
    ang = np.arange(S, dtype=np.float32)[:, None] * inv_freq[None, :]
    cos = np.cos(ang).astype(np.float32).T
    sin = np.sin(ang).astype(np.float32).T
    cosT = np.concatenate([cos, cos], 0)
    sinX = np.concatenate([sin, -sin], 0)
    return np.ascontiguousarray(cosT), np.ascontiguousarray(sinX)


def build_A(stages=("gdn", "att"), fused=False):
    nc = bass.Bass("TRN2", target_bir_lowering=False)
    dt_in = lambda n, s: T(nc.dram_tensor(n, s, F32, kind="ExternalInput").ap(), Buf(n))
    xT = dt_in("xT", [D, S])
    w1 = dt_in("w1", [D, 1792])
    wg = dt_in("wg", [D, 8])
    convw = dt_in("convw", [128, 20])
    alog = dt_in("alog", [128, 4])
    dtb = dt_in("dtb", [128, 4])
    gnw = dt_in("gnw", [128, 128])
    cosT = dt_in("cosT", [128, S])
    sinX = dt_in("sinX", [128, S])
    cst_d = dt_in("cst", [128, C_N])
    if fused:
        yloc = nc.dram_tensor("yloc", [512, S // 2], F32, kind="Internal").ap()
        yall = nc.dram_tensor("yall", [D, S // 2], F32, kind="Internal", addr_space="Shared").ap()
        yT = T(yloc.bitcast(BF16), Buf("yT"))
        din = lambda n, s, dt=F32: T(nc.dram_tensor(n, s, dt, kind="ExternalInput").ap(), Buf(n))
        x_c = din("x_c", [1024, D])
        pT_c = din("pT_c", [256, 1024])
        wo = din("wo", [D, D])
        wgt = din("wgt", [D, D])
        wup = din("wup", [256, D])
        lng = {k: din(k, [128, D]) for k in ("ln1g", "ln1b", "ln2g", "ln2b")}
        lng["ident"] = din("ident", [128, 128])
        out_c = T(nc.dram_tensor("out_c", [1024, D], F32, kind="ExternalOutput").ap(), Buf("out_c"))
    else:
        yT = T(nc.dram_tensor("yT", [512, S], BF16, kind="ExternalOutput").ap(), Buf("yT"))
    U = T(nc.dram_tensor("U", [4, 128, S + 4], F32, kind="Internal").ap(), Buf("U"))
    ZS = T(nc.dram_tensor("ZS", [4, 128, S], BF16, kind="Internal").ap(), Buf("ZS"))
    QKV = T(nc.dram_tensor("QKV", [6, 128, S], BF16, kind="Internal").ap(), Buf("QKV"))

    P = Prog(nc)
    with contextlib.ExitStack() as st:
        arena_t = st.enter_context(nc.sbuf_tensor("arena", [128, ARENA], F32))
        ps_t = st.enter_context(nc.psum_tensor("ps", [128, 4096], F32))
        A = Arena(arena_t, ARENA)
        bank = lambda k: ps_t[:, 512 * k:512 * (k + 1)]

        cst = A.f32(C_N, "cst")
        cstb = A.bf16(C_N, "cstb")
        G = A.f32(64 * 8, "G")
        small = A.f32(20 + 4 + 4 + 128 + 4, "small")
        convw_s = small[:, 0:20]
        alog_s = small[:, 20:24]
        dtb_s = small[:, 24:28]
        gnw_s = small[:, 28:156]
        eps6 = small[:, 156:157]
        zero2 = small[:, 158:160]
        P.dma("sp", cst, cst_d)
        P.dma("sp", convw_s, convw)
        P.dma("sp", alog_s, alog)
        P.dma("sp", dtb_s, dtb)
        P.dma("sp", gnw_s, gnw)
        P.copy("dve", cstb, cst)
        nidentb = A.bf16(128, "nidentb")
        P.ts("dve", nidentb, cstb[:, C_ID:C_ID + 128], -1.0, ALU.mult, 0.0, ALU.add)
        P.memset("dve", eps6, 1e-6)
        P.memset("dve", zero2, 0.0)
        ident32 = cst[:, C_ID:C_ID + 128]
        identb = cstb[:, C_ID:C_ID + 128]
        onesb = cstb[:, C_ONE:C_ONE + 128]
        bandb = cstb[:, C_BAND:C_BAND + 256]
        base_mark = A.p

        TB = 256
        NTB = S // TB
        xTv = xT.re("(kc p) t -> p kc t", p=128)
        w1v = w1.re("(kc p) f -> p kc f", p=128)
        wgv = wg.re("(kc p) f -> p kc f", p=128)
        Wg = [v3(A.bf16(4 * 1792, f"W{i}"), 4) for i in range(8)]
        WG = v3(A.bf16(32 * 8, "WG"), 32)
        XB = [[v3(A.bf16(16 * TB, f"XB{i}{j}"), 16) for j in range(2)] for i in range(2)]
        CS = [v3(A.f32(2 * TB, f"CS{i}"), 2) for i in range(2)]
        stf = [A.f32(TB, f"stf{i}") for i in range(4)]
        stb = [A.bf16(TB, f"stb{i}") for i in range(8)]
        tm1 = [A.f32(TB, f"tm1{i}") for i in range(2)]
        tm2 = [A.f32(TB, f"tm2{i}") for i in range(2)]
        PSF = [T(bank(k)[:, 0:TB], pbuf(f"psf{k}")) for k in range(7)]
        PSG = T(bank(7)[:, 0:16], pbuf("psg"))
        for i in range(8):
            P.dma("pool", Wg[i], w1v[:, 4 * i:4 * i + 4, :])
        P.dma("pool", WG, wgv)
        Uv = U.re("c p t -> p c t")
        zt = A.f32(8, "zt")
        P.memset("pool", zt, 0.0)
        P.dma("sp", Uv[:, :, 0:2], v3(zt, 4))
        P.dma("sp", Uv[:, :, S + 2:S + 4], v3(zt, 4))

        def load_block(tb):
            for j in range(2):
                P.dma("pool", XB[tb % 2][j], xTv[:, 16 * j:16 * j + 16, tb * TB:(tb + 1) * TB])
            P.dma("sp", CS[tb % 2][:, 0, :], cosT[:, tb * TB:(tb + 1) * TB])
            P.dma("sp", CS[tb % 2][:, 1, :], sinX[:, tb * TB:(tb + 1) * TB])

        load_block(0)
        nf = nb = nt = 0
        KCS = list(range(32)) if not FAST else [0]
        for tb in range(NTB):
            if tb + 1 < NTB:
                load_block(tb + 1)
            xb = XB[tb % 2]
            cs = CS[tb % 2]
            tsl = slice(tb * TB, (tb + 1) * TB)
            for ft in range(14):
                ps = PSF[(tb * 14 + ft) % 7]
                for kc in KCS:
                    P.matmul(ps, Wg[kc // 4][:, kc % 4, ft * 128:(ft + 1) * 128], xb[kc // 16][:, kc % 16, :],
                             start=(kc == KCS[0]), stop=(kc == KCS[-1]))
                if ft < 4:
                    s_ = stf[nf % 4]; nf += 1
                    P.copy("act", s_, ps)
                    P.dma("sp", U[ft][:, 2 + tb * TB:2 + (tb + 1) * TB], s_)
                elif ft in (4, 5, 12, 13):
                    s_ = stb[nb % 8]; nb += 1
                    P.act(s_, ps, AF.Silu)
                    zi = ft - 4 if ft < 6 else ft - 10
                    P.dma("sp", ZS[zi][:, tsl], s_)
                elif ft in (10, 11):
                    s_ = stb[nb % 8]; nb += 1
                    P.copy("dve", s_, ps)
                    P.dma("sp", QKV[4 + ft - 10][:, tsl], s_)
                else:
                    t1 = tm1[nt % 2]; t2 = tm2[nt % 2]; nt += 1
                    s_ = stb[nb % 8]; nb += 1
                    P.tt("dve", t1, ps, cs[:, 0, :], ALU.mult)
                    P.tt("dve", t2[0:64, :], ps[64:128, :], cs[64:128, 1, :], ALU.mult)
                    P.tt("dve", t2[64:128, :], ps[0:64, :], cs[0:64, 1, :], ALU.mult)
                    P.tt("pool", s_, t1, t2, ALU.add)
                    P.dma("sp", QKV[ft - 6][:, tsl], s_)
            for hf in range(2):
                for kc in KCS:
                    P.matmul(PSG[:, hf * 8:hf * 8 + 8], xb[kc // 16][:, kc % 16, hf * 128:(hf + 1) * 128],
                             WG[:, kc, :], start=(kc == KCS[0]), stop=(kc == KCS[-1]))
            P.copy("act", G[:, tb * 16:tb * 16 + 16], PSG)
        P.barrier()
        A.p = base_mark

        if "gdn" in stages:
            build_gdn(nc, P, A, bank, locals())
            P.barrier()
            A.p = base_mark
        if "att" in stages:
            build_att(nc, P, A, bank, locals())
        if fused:
            P.barrier()
            A.p = 0
            YALL = T(yall, Buf("yall"))
            P.collective(lambda e, sem: e.collective_compute("AllGather", ALU.bypass, replica_groups=[list(range(NCORES))],
                                                            ins=[yloc], outs=[yall]).then_inc(sem, 1),
                         reads=[yT], writes=[YALL])
            emit_B(nc, P, A, bank, T(yall.bitcast(BF16), YALL.buf), x_c, pT_c, wo, wgt, wup, lng, out_c, dyn=True)
        P.emit()
    return nc


def build_att(nc, P, A, bank, env):
    QKV, ZS, yT = env["QKV"], env["ZS"], env["yT"]
    identb, onesb, bandb = env["identb"], env["onesb"], env["bandb"]
    QT = A.bf16(S, "aQT")
    KT = A.bf16(S, "aKT")
    VT = A.bf16(S, "aVT")
    ZG = A.bf16(S, "aZG")
    OA = A.f32(S, "aOA")
    DN = A.f32(S, "aDN")
    PT = [A.bf16(256, f"aPT{i}") for i in range(3)]
    PM = [A.bf16(256, f"aPM{i}") for i in range(3)]
    VK = [A.bf16(128, f"aVK{i}") for i in range(3)]
    YB = [A.bf16(512, f"aYB{i}") for i in range(2)]
    PS_S = [T(bank(k)[:, 0:256], pbuf(f"pss{k}")) for k in range(2)]
    PS_V = [T(bank(2 + k)[:, 0:64].bitcast(BF16), pbuf(f"psv{k}")) for k in range(2)]
    PS_O = [T(bank(4 + k)[:, 0:256], pbuf(f"pso{k}")) for k in range(2)]
    PS_D = [T(bank(6 + k)[:, 0:256], pbuf(f"psd{k}")) for k in range(2)]
    it = 0
    for h in range(2):
        P.dma("sp", QT, QKV[0 + h])
        P.dma("sp", KT, QKV[2 + h])
        P.dma("sp", VT, QKV[4 + h])
        P.dma("sp", ZG, ZS[2 + h])
        for q4 in range(4):
            P.memset("pool", OA[:, q4 * 2048:(q4 + 1) * 2048], 0.0)
            P.memset("pool", DN[:, q4 * 2048:(q4 + 1) * 2048], 0.0)
        for dil in (1, 4, 16):
            L = S // dil
            for r in range(dil):
                for kt in range(L // 128):
                    wlo = max(0, 128 * kt - 64)
                    whi = min(L, 128 * kt + 192)
                    nW = whi - wlo
                    off = wlo - (128 * kt - 64)
                    k0 = r + dil * 128 * kt
                    ksl = slice(k0, k0 + dil * 127 + 1, dil)
                    q0 = r + dil * wlo
                    qsl = slice(q0, q0 + dil * (nW - 1) + 1, dil)
                    pss = PS_S[it % 2]; psv = PS_V[it % 2]; pso = PS_O[it % 2]; psd = PS_D[it % 2]
                    pt = PT[it % 3]; pm = PM[it % 3]; vk = VK[it % 3]
                    it += 1
                    P.matmul(pss[:, 0:nW], KT[:, ksl], QT[:, qsl])
                    P.transpose(psv, VT[:, ksl], identb)
                    P.act(pt[:, 0:nW], pss[:, 0:nW], AF.Exp, scale=float(128 ** -0.5))
                    P.copy("act", vk, psv)
                    P.tt("pool", pm[:, 0:nW], pt[:, 0:nW], bandb[:, off:off + nW], ALU.mult)
                    P.matmul(pso[:, 0:nW], vk, pm[:, 0:nW])
                    P.matmul(psd[:, 0:nW], onesb, pm[:, 0:nW])
                    P.tt("dve", OA[:, qsl], pso[:, 0:nW], OA[:, qsl], ALU.add)
                    P.tt("dve", DN[:, qsl], psd[:, 0:nW], DN[:, qsl], ALU.add)
        for q in range(16):
            sl = slice(q * 512, (q + 1) * 512)
            yb = YB[q % 2]
            P.recip(DN[:, sl], DN[:, sl])
            P.tt("pool", OA[:, sl], OA[:, sl], DN[:, sl], ALU.mult)
            P.tt("dve", yb, OA[:, sl], ZG[:, sl], ALU.mult)
            P.dma("sp", yT[256 + h * 128:256 + (h + 1) * 128, sl], yb)


def build_gdn(nc, P, A, bank, env):
    U, ZS, yT, G = env["U"], env["ZS"], env["yT"], env["G"]
    cst, cstb = env["cst"], env["cstb"]
    ident32, identb, onesb = env["ident32"], env["identb"], env["onesb"]
    convw_s, alog_s, dtb_s, gnw_s, eps6 = env["convw_s"], env["alog_s"], env["dtb_s"], env["gnw_s"], env["eps6"]
    QT = A.bf16(S, "gQT")
    KT = A.bf16(S, "gKT")
    Ktok = v3(A.bf16(64 * 128, "gKtok"), 64)
    Vtok = v3(A.bf16(64 * 256, "gVtok"), 64)
    OtA = A.f32(64 * 256, "gOt")
    Ot = [T(OtA.ap[:, t * 256:(t + 1) * 256], Buf(f"Ot{t}")) for t in range(64)]
    work_mark = A.p
    mini = env.get("mini")
    if mini:
        return build_gdn_core(nc, P, A, bank, env, locals())

    UB = [A.f32(516, f"ub{i}") for i in range(2)]
    ACC = [A.f32(512, f"acc{i}") for i in range(2)]
    SB = [A.f32(512, f"sb{i}") for i in range(2)]
    SQ = [A.bf16(512, f"sq{i}") for i in range(2)]
    RS = [A.f32(512, f"rs{i}") for i in range(2)]
    PSN = [T(bank(k), pbuf(f"psn{k}")) for k in range(2)]
    PSTF = [T(bank(2 + k), pbuf(f"pstf{k}")) for k in range(2)]
    PSTB = [T(bank(4 + k)[:, 0:256].bitcast(BF16), pbuf(f"pstb{k}")) for k in range(2)]
    PCV = [T(bank(6 + k), pbuf(f"pscv{k}")) for k in range(2)]
    UBB = [A.bf16(516, f"ubb{i}") for i in range(2)]
    DG = v3(A.bf16(20 * 128, "convdiag"), 20)
    for j in range(20):
        P.ts("dve", DG[:, j, :], identb, convw_s[:, j:j + 1], ALU.mult, 0.0, ALU.add)
    it = 0
    for ct in range(4):
        for tb in range(16):
            ub, acc, sb, sq, rs = UB[it % 2], ACC[it % 2], SB[it % 2], SQ[it % 2], RS[it % 2]
            eng = "dve"
            P.dma("sp", ub, U[ct][:, tb * 512:tb * 512 + 516])
            ubb = UBB[it % 2]
            pcv = PCV[it % 2]
            P.copy("pool", ubb, ub)
            for k in range(5):
                P.matmul(pcv, DG[:, ct * 5 + k, :], ubb[:, k:k + 512], start=(k == 0), stop=(k == 4))
            P.act(sb, pcv, AF.Silu)
            sl = slice(tb * 512, (tb + 1) * 512)
            if ct < 2:
                P.tt("pool", sq, sb, sb, ALU.mult)
                psn = PSN[it % 2]
                P.matmul(psn, onesb, sq)
                P.act(rs, psn, AF.Sqrt, bias=eps6)
                P.recip(rs, rs)
                dst = QT if ct == 0 else KT
                P.stt("dve", dst[:, sl], sb, float(128 ** -0.5) if ct == 0 else 1.0, rs, ALU.mult, ALU.mult)
                if ct == 1:
                    pst = PSTB[it % 2]
                    for j in range(4):
                        P.transpose(pst[:, j * 128:(j + 1) * 128], KT[:, tb * 512 + j * 128:tb * 512 + (j + 1) * 128], identb)
                    P.copy("act", Ktok[:, 4 * tb:4 * tb + 4, :], v3(pst, 4))
            else:
                h = ct - 2
                pst = PSTF[it % 2]
                for j in range(4):
                    P.transpose(pst[:, j * 128:(j + 1) * 128], sb[:, j * 128:(j + 1) * 128], ident32)
                P.copy("act", Vtok[:, 4 * tb:4 * tb + 4, h * 128:(h + 1) * 128], v3(pst, 4))
            it += 1
    P.barrier()
    A.p = work_mark
    if GDN_STOP == 0:
        return

    g4 = v3(A.f32(256, "g4"), 64)
    beta4 = v3(A.f32(256, "beta4"), 64)
    tmpg = v3(A.f32(256, "tmpg"), 64)
    nA = A.f32(4, "nA")
    G3 = v3(G, 64)
    P.act(nA, alog_s, AF.Exp)
    P.ts("dve", nA, nA, -1.0, ALU.mult, 0.0, ALU.add)
    for s_ in range(4):
        P.ts("dve", tmpg[:, :, s_], G3[:, :, s_], dtb_s[:, s_:s_ + 1], ALU.add, 0.0, ALU.add)
    P.act(tmpg, tmpg, AF.Exp)
    P.act(tmpg, tmpg, AF.Ln, bias=1.0)
    for s_ in range(4):
        P.ts("dve", g4[:, :, s_], tmpg[:, :, s_], nA[:, s_:s_ + 1], ALU.mult, 0.0, ALU.add)
    P.act(beta4, G3[:, :, 4:8], AF.Sigmoid)
    P.barrier()
    if GDN_STOP == 1:
        return
    return build_gdn_core(nc, P, A, bank, env, locals())


def build_gdn_core(nc, P, A, bank, env, loc):
    U, ZS, yT, G = env["U"], env["ZS"], env["yT"], env["G"]
    cst, cstb = env["cst"], env["cstb"]
    ident32, identb, onesb = env["ident32"], env["identb"], env["onesb"]
    gnw_s, eps6 = env["gnw_s"], env["eps6"]
    QT, KT, Ktok, Vtok, Ot, work_mark = (loc[k] for k in ("QT", "KT", "Ktok", "Vtok", "Ot", "work_mark"))
    env["Ot"] = Ot
    mini = env.get("mini")
    if mini:
        rnd = env["rnd"]
        g4 = v3(A.f32(256, "g4"), 64)
        beta4 = v3(A.f32(256, "beta4"), 64)
        rs_ = T(loc["OtA"].ap[:, 8192:12288], Buf("rnds"))
        P.dma("sp", rs_, rnd)
        P.copy("dve", QT[:, 0:256], rs_[:, 0:256])
        P.copy("dve", KT[:, 0:256], rs_[:, 256:512])
        P.copy("dve", Ktok[:, 0:2, :], v3(rs_[:, 512:768], 2))
        P.copy("dve", Vtok[:, 0:2, :], v3(rs_[:, 768:1280], 2))
        P.copy("dve", g4[:, 0:2, :], v3(rs_[:, 1280:1288], 2))
        P.copy("dve", beta4[:, 0:2, :], v3(rs_[:, 1288:1296], 2))
        P.barrier()
    else:
        g4, beta4 = loc["g4"], loc["beta4"]

    KQm = [A.f32(256, f"KQm{d}") for d in range(2)]
    Lg = [A.f32(128, f"Lg{s_}") for s_ in range(4)]
    E = [A.f32(128, f"E{s_}") for s_ in range(4)]
    STa = [[A.bf16(512, f"ST{s_}{k}") for k in range(2)] for s_ in range(4)]
    nidentb = env["nidentb"]
    AQ = [[A.bf16(256, f"AQ{p}{s_}") for s_ in range(4)] for p in range(2)]
    Y0 = [A.bf16(128, f"Y0{s_}") for s_ in range(4)]
    TTb = [[A.bf16(128, f"TTb{p}{s_}") for s_ in range(4)] for p in range(2)]
    ktb = [[A.bf16(128, f"ktb{p}{s_}") for s_ in range(4)] for p in range(2)]
    sc = [[A.f32(16, f"sc{p}{d}") for d in range(2)] for p in range(2)]
    gsel = [A.f32(4, f"gsel{d}") for d in range(2)]
    Sbf = [A.bf16(256, f"Sbf{d}") for d in range(2)]
    S32 = [A.f32(256, f"S32{d}") for d in range(2)]
    rb = [A.bf16(256, f"rb{d}") for d in range(2)]
    qs = [A.f32(256, f"qs{d}") for d in range(2)]
    vn = [A.bf16(256, f"vn{d}") for d in range(2)]
    tmpo = [A.f32(256, f"tmpo{d}") for d in range(2)]
    for d in range(2):
        P.memset("pool", gsel[d], 0.0)
        P.memset("pool", Sbf[d], 0.0)
        P.memset("pool", S32[d], 0.0)
    bb = [pbuf(f"gbank{k}") for k in range(8)]
    PKS = [T(bank(2 * d)[:, 0:256], bb[2 * d]) for d in range(2)]
    PSU = [T(bank(2 * d)[:, 256:512], bb[2 * d]) for d in range(2)]
    PQS = [T(bank(2 * d + 1)[:, 0:256], bb[2 * d + 1]) for d in range(2)]
    PVO = [T(bank(2 * d + 1)[:, 256:512], bb[2 * d + 1]) for d in range(2)]
    PK = T(bank(4)[:, 0:256], bb[4])
    PSM = [T(bank(4)[:, 256 + 8 * d:264 + 8 * d], bb[4]) for d in range(2)]
    PD = [T(bank(5)[:, 128 * h:128 * h + 128], bb[5]) for h in range(2)]
    PCB = [T(bank(5)[:, 256 + 64 * h:320 + 64 * h].bitcast(BF16), bb[5]) for h in range(2)]
    PCH = [T(bank(6 + h), bb[6 + h]) for h in range(2)]
    mLT, mLE, mGT, mGE = (cst[:, o:o + 128] for o in (C_LT, C_LE, C_GT, C_GE))
    mBO, mONE = cst[:, C_BO:C_BO + 128], cst[:, C_ONE:C_ONE + 128]
    MstrictT = [mLT, mGT]
    Mincl = [mGE, mLE]
    Mcat = [cst[:, C_GT:C_GT + 256], cst[:, C_LT:C_LT + 256]]

    def g1(tile, d, par):
        kt = KT[:, tile * 128:(tile + 1) * 128]
        qt = QT[:, tile * 128:(tile + 1) * 128]
        scd = sc[par][d]
        gcol = g4[:, tile, 2 * d:2 * d + 2]
        bcol = beta4[:, tile, 2 * d:2 * d + 2]
        P.matmul(PK[:, 0:128], kt, kt)
        P.matmul(PK[:, 128:256], kt, qt)
        P.tt("dve", KQm[d], PK, Mcat[d], ALU.mult)
        yield
        psm = PSM[d]
        P.matmul(psm[:, 0:2], MstrictT[d], gcol)
        P.matmul(psm[:, 2:4], mBO, gcol)
        gs3 = v3(gsel[d], 2)
        P.copy("pool", gs3[0:64, :, 0], gcol[0:64, :])
        P.copy("pool", gs3[64:128, :, 1], gcol[64:128, :])
        P.matmul(psm[:, 4:8], mONE, gsel[d])
        P.copy("act", scd[:, 0:2], psm[:, 0:2])
        P.tt("dve", scd[:, 2:4], psm[:, 2:4], scd[:, 0:2], ALU.subtract)
        P.act(scd[:, 4:6], scd[:, 2:4], AF.Exp)
        P.ts("dve", scd[:, 6:8], scd[:, 4:6], -1.0, ALU.mult, 0.0, ALU.add)
        P.act(scd[:, 8:10], scd[:, 0:2], AF.Exp)
        P.tt("dve", scd[:, 8:10], scd[:, 8:10], bcol, ALU.mult)
        P.act(scd[:, 10:14], psm[:, 4:8], AF.Exp)
        yield
        for h in range(2):
            s_ = 2 * d + h
            b1 = beta4[:, tile, s_:s_ + 1]
            P.ts("pool", Lg[s_], Mincl[d], g4[:, tile, s_:s_ + 1], ALU.mult, 0.0, ALU.add)
            P.matmul(PD[d], MstrictT[d], Lg[s_])
            P.act(E[s_], PD[d], AF.Exp)
            aq = AQ[par][s_]
            P.stt("dve", v3(aq, 2), T(E[s_].ap.unsqueeze(1).to_broadcast([128, 2, 128]), E[s_].buf), b1,
                  v3(KQm[d], 2), ALU.mult, ALU.mult)
            P.ts("pool", ktb[par][s_], Ktok[:, tile, :], scd[:, 8 + h:9 + h], ALU.mult, 0.0, ALU.add)
            yield
            X0 = aq[:, 0:128]
            pcb = PCB[d]
            pch = PCH[d]
            ev = "act" if d == 0 else "dve"
            bc2 = lambda t_: T(t_.ap.unsqueeze(1).to_broadcast([128, 2, 128]), t_.buf)
            P.transpose(pcb, X0, identb)
            P.copy("act", Y0[s_], pcb)
            P.matmul(pch[:, 0:128], Y0[s_], X0)
            P.matmul(pch[:, 128:256], identb, identb, start=True, stop=False)
            P.matmul(pch[:, 128:256], nidentb, X0, start=False, stop=True)
            P.matmul(v3(pch[:, 256:512], 2), X0, bc2(Y0[s_]), start=True, stop=False)
            P.matmul(pch[:, 384:512], identb, identb, start=False, stop=True)
            P.copy(ev, STa[s_][1], pch)
            yield
            for k in range(1, 5):
                cur = STa[s_][k % 2]
                nxt = STa[s_][(k + 1) % 2]
                P.matmul(pch[:, 0:128], cur[:, 256:384], cur[:, 0:128])
                P.matmul(pch[:, 128:256], cur[:, 384:512], cur[:, 128:256])
                P.matmul(v3(pch[:, 256:512], 2), cur[:, 0:128], bc2(cur[:, 256:384]), start=True, stop=False)
                P.matmul(pch[:, 384:512], identb, identb, start=False, stop=True)
                P.copy(ev, nxt, pch)
                yield
            cur = STa[s_][1]
            P.matmul(pch[:, 128:256], cur[:, 384:512], cur[:, 128:256])
            P.copy(ev, TTb[par][s_], pch[:, 128:256])
            yield

    def g2(tile, d, par, first):
        kt = KT[:, tile * 128:(tile + 1) * 128]
        qt = QT[:, tile * 128:(tile + 1) * 128]
        scd = sc[par][d]
        for c in ((0, 1) if d == 0 else (1, 0)):
            R = slice(64 * c, 64 * c + 64)
            P.matmul(PKS[d], kt, Sbf[d])
            P.matmul(PQS[d], qt, Sbf[d])
            for h in range(2):
                hs = slice(h * 128, (h + 1) * 128)
                P.stt("dve", rb[d][R, hs], PKS[d][R, hs], scd[R, 6 + h:7 + h], Vtok[R, tile, hs], ALU.mult, ALU.add)
            for h in range(2):
                hs = slice(h * 128, (h + 1) * 128)
                P.act(qs[d][R, hs], PQS[d][R, hs], AF.Identity, scale=scd[R, 4 + h:5 + h])
            yield
            for h in range(2):
                hs = slice(h * 128, (h + 1) * 128)
                P.matmul(PVO[d][:, hs], TTb[par][2 * d + h][R, :], rb[d][R, hs])
            P.copy("act", vn[d][R, :], PVO[d][R, :])
            yield
            for h in range(2):
                hs = slice(h * 128, (h + 1) * 128)
                P.matmul(PSU[d][:, hs], ktb[par][2 * d + h][R, :], vn[d][R, hs])
            for h in range(2):
                hs = slice(h * 128, (h + 1) * 128)
                P.matmul(PVO[d][:, hs], AQ[par][2 * d + h][R, 128:256], vn[d][R, hs])
            for h in range(2):
                hs = slice(h * 128, (h + 1) * 128)
                eg = scd[:, 10 + 2 * h + c:11 + 2 * h + c]
                P.stt("dve", Sbf[d][:, hs], S32[d][:, hs], eg, PSU[d][:, hs], ALU.mult, ALU.add)
            for h in range(2):
                hs = slice(h * 128, (h + 1) * 128)
                eg = scd[:, 10 + 2 * h + c:11 + 2 * h + c]
                P.stt("dve", S32[d][:, hs], S32[d][:, hs], eg, PSU[d][:, hs], ALU.mult, ALU.add)
            if first:
                P.tt("dve", Ot[tile][R, :], PVO[d][R, :], qs[d][R, :], ALU.add)
            else:
                P.tt("dve", tmpo[d][R, :], PVO[d][R, :], qs[d][R, :], ALU.add)
                P.tt("pool", Ot[tile][R, :], Ot[tile][R, :], tmpo[d][R, :], ALU.add)
            yield

    def run_round_robin(gens):
        gens = list(gens)
        while gens:
            for g_ in list(gens):
                try:
                    next(g_)
                except StopIteration:
                    gens.remove(g_)

    NT = 64 if not mini else 2
    if GDN_STOP == 2:
        import itertools
        n1 = int(os.environ.get('G1_STOP', '99'))
        for g_ in (g1(0, 0, 0), g1(NT - 1, 1, 0)):
            for _ in itertools.islice(g_, n1):
                pass
        return
    run_round_robin([g1(0, 0, 0), g1(NT - 1, 1, 0)])
    for i in range(NT):
        par = i % 2
        gens = [g2(i, 0, par, i < NT // 2), g2(NT - 1 - i, 1, par, i < NT // 2)]
        if i + 1 < NT:
            gens += [g1(i + 1, 0, 1 - par), g1(NT - 2 - i, 1, 1 - par)]
        run_round_robin(gens)
    P.barrier()
    A.p = work_mark
    if GDN_STOP == 3 or mini:
        return

    sqb = [A.f32(256, f"g3sq{i}") for i in range(2)]
    ssb = [A.f32(2, f"g3ss{i}") for i in range(2)]
    onb = [A.f32(256, f"g3on{i}") for i in range(2)]
    zg = [v3(A.bf16(1024, f"g3zg{i}"), 2) for i in range(2)]
    yb = [v3(A.bf16(256, f"g3yb{i}"), 2) for i in range(2)]
    PT3 = [T(bank(k)[:, 0:256], pbuf(f"pt3{k}")) for k in range(2)]
    yTv = yT[0:256, :].re("(h p) t -> p h t", h=2)
    ZSv = T(ZS.ap[0:2].rearrange("h p t -> p h t"), ZS.buf)
    for tile in range(NT):
        i2 = tile % 2
        if tile % 4 == 0:
            zt_ = zg[(tile // 4) % 2]
            P.dma("sp", zt_, ZSv[:, :, tile * 128:tile * 128 + 512])
        zt_ = zg[(tile // 4) % 2]
        o = Ot[tile]
        P.tt("pool", sqb[i2], o, o, ALU.mult)
        P.rsum(ssb[i2], v3(sqb[i2], 2))
        P.act(ssb[i2], ssb[i2], AF.Sqrt, bias=eps6, scale=1.0 / 128.0)
        P.recip(ssb[i2], ssb[i2])
        for h in range(2):
            hs = slice(h * 128, (h + 1) * 128)
            P.stt("dve", onb[i2][:, hs], o[:, hs], ssb[i2][:, h:h + 1], gnw_s, ALU.mult, ALU.mult)
            P.transpose(PT3[i2][:, hs], onb[i2][:, hs], ident32)
        P.tt("dve", yb[i2], v3(PT3[i2], 2), zt_[:, :, (tile % 4) * 128:(tile % 4) * 128 + 128], ALU.mult)
        P.dma("sp", yTv[:, :, tile * 128:(tile + 1) * 128], yb[i2])


def core_cols(c):
    o = OFFS
    cols = np.concatenate([
        np.arange(o[0] + 128 * c, o[0] + 128 * c + 128),
        np.arange(o[1] + 128 * c, o[1] + 128 * c + 128),
        np.arange(o[2] + 256 * c, o[2] + 256 * c + 256),
        np.arange(o[3] + 256 * c, o[3] + 256 * c + 256),
        np.arange(o[6] + 256 * c, o[6] + 256 * c + 256),
        np.arange(o[7] + 256 * c, o[7] + 256 * c + 256),
        np.arange(o[8] + 256 * c, o[8] + 256 * c + 256),
        np.arange(o[9] + 256 * c, o[9] + 256 * c + 256)])
    gc = np.array([o[4] + 2 * c, o[4] + 2 * c + 1, o[4] + 16 + 2 * c, o[4] + 16 + 2 * c + 1,
                   o[5] + 2 * c, o[5] + 2 * c + 1, o[5] + 16 + 2 * c, o[5] + 16 + 2 * c + 1])
    return cols, gc


def prep_A(inputs):
    x = np.asarray(inputs["x"], np.float32)[0]
    w_in = np.asarray(inputs["w_in"], np.float32)[0]
    conv_w = np.asarray(inputs["conv_w"], np.float32)[0]
    a_log = np.asarray(inputs["a_log"], np.float32)[0]
    dt_bias = np.asarray(inputs["dt_bias"], np.float32)[0]
    gnw = np.asarray(inputs["gdn_norm_w"], np.float32)[0]
    xT = np.ascontiguousarray(x.T)
    cosT, sinX = rope_tables()
    cst = make_consts()
    maps = []
    for c in range(NCORES):
        cols, gcols = core_cols(c)
        ch = np.concatenate([np.arange(128 * c, 128 * c + 128), 1024 + np.arange(128 * c, 128 * c + 128),
                             2048 + np.arange(256 * c, 256 * c + 256)])
        cw = conv_w[:, ch].reshape(5, 4, 128).transpose(2, 1, 0).reshape(128, 20)
        st = [(0, 2 * c), (0, 2 * c + 1), (1, 2 * c), (1, 2 * c + 1)]
        al = np.array([a_log[d, h] for d, h in st], np.float32)
        db = np.array([dt_bias[d, h] for d, h in st], np.float32)
        maps.append({
            "xT": xT,
            "w1": np.ascontiguousarray(w_in[:, cols]),
            "wg": np.ascontiguousarray(w_in[:, gcols]),
            "convw": np.ascontiguousarray(cw),
            "alog": np.ascontiguousarray(np.broadcast_to(al, (128, 4))),
            "dtb": np.ascontiguousarray(np.broadcast_to(db, (128, 4))),
            "gnw": np.ascontiguousarray(np.broadcast_to(gnw, (128, 128))),
            "cosT": cosT, "sinX": sinX, "cst": cst,
        })
    return maps


def run_A(inputs, stages=("gdn", "att")):
    nc = build_A(stages)
    res = run_bass_kernel_spmd(nc, prep_A(inputs), core_ids=list(range(NCORES)))
    return [np.asarray(r["yT"]) for r in res.results]


def build_mini():
    nc = bass.Bass("TRN2", target_bir_lowering=False)
    rnd = T(nc.dram_tensor("rnd", [128, 4096], F32, kind="ExternalInput").ap())
    cst_d = T(nc.dram_tensor("cst", [128, C_N], F32, kind="ExternalInput").ap())
    ot = T(nc.dram_tensor("ot", [128, 512], F32, kind="ExternalOutput").ap())
    P = Prog(nc)
    with contextlib.ExitStack() as st:
        arena_t = st.enter_context(nc.sbuf_tensor("arena", [128, ARENA], F32))
        ps_t = st.enter_context(nc.psum_tensor("ps", [128, 4096], F32))
        A = Arena(arena_t, ARENA)
        bank = lambda k: ps_t[:, 512 * k:512 * (k + 1)]
        cst = A.f32(C_N, "cst")
        cstb = A.bf16(C_N, "cstb")
        small = A.f32(160, "small")
        P.dma("sp", cst, cst_d)
        P.copy("dve", cstb, cst)
        nidentb = A.bf16(128, "nidentb")
        P.ts("dve", nidentb, cstb[:, C_ID:C_ID + 128], -1.0, ALU.mult, 0.0, ALU.add)
        env = dict(nidentb=nidentb, U=None, ZS=None, yT=None, G=None, cst=cst, cstb=cstb, ident32=cst[:, C_ID:C_ID + 128],
                   identb=cstb[:, C_ID:C_ID + 128], onesb=cstb[:, C_ONE:C_ONE + 128], convw_s=None, alog_s=None,
                   dtb_s=None, gnw_s=None, eps6=None, mini=True, rnd=rnd)
        loc = build_gdn(nc, P, A, bank, env)
        P.barrier()
        if GDN_STOP != 2:
            for t in range(2):
                P.dma("sp", ot[:, t * 256:(t + 1) * 256], env["Ot"][t])
        P.emit()
    return nc


ALPHA = float(2.0 ** 0.25)


def emit_B(nc, P, A, bank, yTa, x_c, pT_c, wo, wgt, wup, lng, out_c, dyn=False):
    NB = 512
    NNB = D // NB
    YT = v3(A.bf16(32 * 512, "bYT"), 32)
    H = v3(A.f32(4 * D, "bH"), 4)
    WB = [v3(A.bf16(32 * NB, f"bWB{i}"), 32) for i in range(2)]
    WU = [v3(A.bf16(2 * NB, f"bWU{i}"), 2) for i in range(2)]
    PTb = v3(A.bf16(2 * 512, "bPT"), 2)
    XS = [v3(A.f32(4 * NB, f"bXS{i}"), 4) for i in range(1)]
    GB = [[A.f32(512, f"bGB{i}{j}") for j in range(2)] for i in range(1)]
    SG = [A.f32(NB, f"bSG{i}") for i in range(2)]
    PL = [A.f32(NB, f"bPL{i}") for i in range(2)]
    junk = A.bf16(512, "bjunk")
    stat = A.f32(64, "bstat")
    epsln = A.f32(1, "bepsln")
    ident32 = A.f32(128, "bident")
    P.memset("dve", epsln, 1e-5)
    P.memset("pool", ident32, 0.0)
    P.dma("sp", ident32, lng["ident"])
    PSB = [T(bank(k)[:, 0:NB], pbuf(f"psb{k}")) for k in range(8)]
    assert NB <= 512
    PSX = [T(bank(k), pbuf(f"psx{k}")) for k in range(8)]
    yv = yTa.re("(kc p) t -> p kc t", p=128)
    wov = wo.re("(kc p) n -> p kc n", p=128)
    wgv = wgt.re("(kc p) n -> p kc n", p=128)
    wuv = wup.re("(kc p) n -> p kc n", p=128)
    pv = pT_c.re("(kc p) t -> p kc t", p=128)
    xv = x_c.re("(tt p) n -> p tt n", p=128)
    ov = out_c.re("(tt p) n -> p tt n", p=128)
    cnt = [0]

    def layer_norm(gname, bname):
        for tt in range(4):
            h = H[:, tt, :]
            st_ = stat[:, tt * 16:(tt + 1) * 16]
            P.rsum(st_[:, 0:1], h)
            for j in range(8):
                P.add("act", lambda e, j=j, h=h, st_=st_: e.activation(junk.ap, h.ap[:, j * 512:(j + 1) * 512], AF.Square,
                                                                     accum_out=st_.ap[:, 2 + j:3 + j]),
                      reads=[h], writes=[junk, st_])
            P.rsum(st_[:, 1:2], st_[:, 2:10])
            P.ts("dve", st_[:, 10:11], st_[:, 0:1], 1.0 / D, ALU.mult, 0.0, ALU.add)
            P.ts("dve", st_[:, 11:12], st_[:, 1:2], 1.0 / D, ALU.mult, 0.0, ALU.add)
            P.tt("dve", st_[:, 12:13], st_[:, 10:11], st_[:, 10:11], ALU.mult)
            P.tt("dve", st_[:, 11:12], st_[:, 11:12], st_[:, 12:13], ALU.subtract)
            P.act(st_[:, 13:14], st_[:, 11:12], AF.Sqrt, bias=epsln)
            P.recip(st_[:, 13:14], st_[:, 13:14])
            P.ts("dve", h, h, st_[:, 10:11], ALU.subtract, st_[:, 13:14], ALU.mult)
        for blk in range(8):
            gb = GB[0]
            sl = slice(blk * 512, (blk + 1) * 512)
            P.dma("sp", gb[0], lng[gname][:, sl])
            P.dma("sp", gb[1], lng[bname][:, sl])
            for tt in range(4):
                P.tt("pool", H[:, tt, sl], H[:, tt, sl], gb[0], ALU.mult)
                P.tt("dve", H[:, tt, sl], H[:, tt, sl], gb[1], ALU.add)

    for hp in range(2):
        t0 = hp * 512
        if dyn:
            P.add("pool", lambda e, sem, t0=t0: e.dma_start(
                out=YT.ap, in_=yv.ap[:, :, bass.ds(e.partition_id() * 1024 + t0, 512)]).then_inc(sem, 16),
                reads=[yTa], writes=[YT], dma=1)
        else:
            P.dma("sp", YT, yv[:, :, t0:t0 + 512])
        for nb in range(NNB):
            wb = WB[cnt[0] % 2]; xs = XS[0]; cnt[0] += 1
            nsl = slice(nb * NB, (nb + 1) * NB)
            P.dma("pool", wb, wov[:, :, nsl])
            P.dma("sp", xs, xv[:, 4 * hp:4 * hp + 4, nsl])
            for tt in range(4):
                ps = PSB[(nb * 4 + tt) % 8]
                for kc in range(32):
                    P.matmul(ps, YT[:, kc, tt * 128:(tt + 1) * 128], wb[:, kc, :], start=(kc == 0), stop=(kc == 31))
                P.stt("dve", H[:, tt, nsl], xs[:, tt, :], ALPHA, ps, ALU.mult, ALU.add)
        layer_norm("ln1g", "ln1b")
        for kc in range(32):
            px = PSX[kc % 8]
            for tt in range(4):
                P.transpose(px[:, tt * 128:(tt + 1) * 128], H[:, tt, kc * 128:(kc + 1) * 128], ident32)
            P.copy("act" if kc % 2 == 0 else "dve", YT[:, kc, :], px)
        P.dma("pool", PTb, pv[:, :, t0:t0 + 512])
        for nb in range(NNB):
            wb = WB[cnt[0] % 2]; wu = WU[cnt[0] % 2]; cnt[0] += 1
            nsl = slice(nb * NB, (nb + 1) * NB)
            P.dma("pool", wb, wgv[:, :, nsl])
            P.dma("pool", wu, wuv[:, :, nsl])
            for tt in range(4):
                pg = PSB[tt]
                pu = PSB[4 + tt]
                tsl = slice(tt * 128, (tt + 1) * 128)
                for kc in range(32):
                    P.matmul(pg, YT[:, kc, tsl], wb[:, kc, :], start=(kc == 0), stop=(kc == 31))
                for kc in range(2):
                    P.matmul(pu, PTb[:, kc, tsl], wu[:, kc, :], start=(kc == 0), stop=(kc == 1))
                sg = SG[tt % 2]; pl = PL[tt % 2]
                P.act(sg, pg, AF.Sigmoid)
                P.tt("dve", pl, pu, sg, ALU.mult)
                P.stt("dve", H[:, tt, nsl], H[:, tt, nsl], ALPHA, pl, ALU.mult, ALU.add)
        layer_norm("ln2g", "ln2b")
        for tt in range(4):
            P.dma("sp", ov[:, 4 * hp + tt, :], H[:, tt, :])


def build_B():
    nc = bass.Bass("TRN2", target_bir_lowering=False)
    din = lambda n, s, dt=F32: T(nc.dram_tensor(n, s, dt, kind="ExternalInput").ap(), Buf(n))
    yTa = din("yTa", [D, 1024], BF16)
    x_c = din("x_c", [1024, D])
    pT_c = din("pT_c", [256, 1024])
    wo = din("wo", [D, D])
    wgt = din("wgt", [D, D])
    wup = din("wup", [256, D])
    lng = {k: din(k, [128, D]) for k in ("ln1g", "ln1b", "ln2g", "ln2b")}
    lng["ident"] = din("ident", [128, 128])
    out_c = T(nc.dram_tensor("out_c", [1024, D], F32, kind="ExternalOutput").ap(), Buf("out_c"))
    P = Prog(nc)
    with contextlib.ExitStack() as st:
        arena_t = st.enter_context(nc.sbuf_tensor("arena", [128, ARENA], F32))
        ps_t = st.enter_context(nc.psum_tensor("ps", [128, 4096], F32))
        A = Arena(arena_t, ARENA)
        bank = lambda k: ps_t[:, 512 * k:512 * (k + 1)]
        emit_B(nc, P, A, bank, yTa, x_c, pT_c, wo, wgt, wup, lng, out_c)
        P.emit()
    return nc


def prep_B(inputs, yTs):
    x = np.asarray(inputs["x"], np.float32)[0]
    p = np.asarray(inputs["p"], np.float32)[0, 0]
    w_out = np.asarray(inputs["w_out"], np.float32)[0]
    wgt = np.ascontiguousarray(np.asarray(inputs["w_ple_gate"], np.float32)[0])
    wup = np.ascontiguousarray(np.asarray(inputs["w_ple_up"], np.float32)[0])
    rows = np.concatenate([np.concatenate([np.arange(256 * r, 256 * r + 256), 2048 + np.arange(256 * r, 256 * r + 256)])
                           for r in range(NCORES)])
    wo = np.ascontiguousarray(w_out[rows])
    yall = np.concatenate(yTs, 0) if yTs is not None else None
    bc = lambda k: np.ascontiguousarray(np.broadcast_to(np.asarray(inputs[k], np.float32)[0], (128, D)))
    lnp = {"ln1g": bc("ln1_g"), "ln1b": bc("ln1_b"), "ln2g": bc("ln2_g"), "ln2b": bc("ln2_b")}
    ident = np.eye(128, dtype=np.float32)
    pT = np.ascontiguousarray(p.T)
    maps = []
    for c in range(NCORES):
        sl = slice(1024 * c, 1024 * (c + 1))
        m = {"x_c": np.ascontiguousarray(x[sl]), "pT_c": np.ascontiguousarray(pT[:, sl]), "wo": wo, "wgt": wgt,
             "wup": wup, "ident": ident, **lnp}
        if yall is not None:
            m["yTa"] = np.ascontiguousarray(yall[:, sl])
        maps.append(m)
    return maps


def kernel_unfused(**inputs):
    yTs = run_A(inputs)
    ncB = build_B()
    res = run_bass_kernel_spmd(ncB, prep_B(inputs, yTs), core_ids=list(range(NCORES)))
    out = np.concatenate([np.asarray(r["out_c"]) for r in res.results], 0)
    return out.reshape(1, S, D).astype(np.float32)


def kernel(**inputs):
    nc = build_A(fused=True)
    mA = prep_A(inputs)
    mB = prep_B(inputs, None)
    maps = [{**a, **b} for a, b in zip(mA, mB)]
    res = run_bass_kernel_spmd(nc, maps, core_ids=list(range(NCORES)))
    out = np.concatenate([np.asarray(r["out_c"]) for r in res.results], 0)
    return out.reshape(1, S, D).astype(np.float32)
```
